# Optimizing a Trainium2 kernel written in Bass

```python
import jax, jax.numpy as jnp
from jax import lax
import numpy as np

D_MODEL = 1024
BATCH = 2
SEQ = 16384
DEPTH = 4

N_MEM = 256
XA_HEADS = 4
XA_HEAD_DIM = D_MODEL // XA_HEADS
GM_CHUNK = 128
GM_INNER = D_MODEL
GM_GROUPS = 8
GM_GROUP_DIM = GM_INNER // GM_GROUPS
RW_HEAD_DIM = 64
RW_HEADS = D_MODEL // RW_HEAD_DIM
RW_DECAY_LORA = max(32, int(round(1.8 * D_MODEL ** 0.5 / 32)) * 32)
RW_AAA_LORA = max(32, int(round(1.8 * D_MODEL ** 0.5 / 32)) * 32)
RW_MV_LORA = max(32, int(round(1.3 * D_MODEL ** 0.5 / 32)) * 32)
RW_GATE_LORA = max(32, int(round(0.6 * D_MODEL ** 0.8 / 32)) * 32)
RW_GN_EPS = 64e-5
NORM_EPS = 1e-12
D_FF = -(-8 * D_MODEL // (3 * 256)) * 256
RMS_EPS = 1e-6
LN_EPS = 1e-5

kernel_name = "gmlp_rwkv7_interleaved_memxattn_sandwich"


def rms_norm(x, g):
    xf = x.astype(jnp.float32)
    y = xf * lax.rsqrt(jnp.mean(xf * xf, axis=-1, keepdims=True) + RMS_EPS)
    return (y * g.astype(jnp.float32)).astype(x.dtype)


def layer_norm(x, g, b):
    xf = x.astype(jnp.float32)
    mu = jnp.mean(xf, axis=-1, keepdims=True)
    var = jnp.mean(jnp.square(xf - mu), axis=-1, keepdims=True)
    y = (xf - mu) * lax.rsqrt(var + LN_EPS) * g.astype(jnp.float32) + b.astype(jnp.float32)
    return y.astype(x.dtype)


def gmlp_spatial_gating(x, w_in, b_in, ln_g, ln_b, w_s, b_s, w_out):
    B, S, _ = x.shape
    h = jax.nn.gelu(x @ w_in + b_in, approximate=False)
    u, v = jnp.split(h, 2, axis=-1)
    v = layer_norm(v, ln_g, ln_b)
    v = v.reshape(B, S // GM_CHUNK, GM_CHUNK, GM_GROUPS, GM_GROUP_DIM)
    w_causal = jnp.tril(w_s)
    mixed = jnp.einsum('gts,bnsgc->bntgc', w_causal, v) + b_s.T[:, :, None]
    return (u * mixed.reshape(B, S, GM_INNER)) @ w_out


def token_shift(x):
    return jnp.pad(x[:, :-1], ((0, 0), (1, 0), (0, 0)))


def rwkv7_recurrence(r, w, k, v, a, b):
    B, T, H, N = r.shape

    def step(S, inp):
        r_t, w_t, k_t, v_t, a_t, b_t = inp
        sa = jnp.einsum('bhij,bhj->bhi', S, a_t)
        S = S * w_t[:, :, None, :] + sa[..., None] * b_t[:, :, None, :] + v_t[..., None] * k_t[:, :, None, :]
        return S, jnp.einsum('bhij,bhj->bhi', S, r_t)

    xs = tuple(jnp.moveaxis(z, 1, 0) for z in (r, w, k, v, a, b))
    S0 = jnp.zeros((B, H, N, N), jnp.float32)
    _, y = lax.scan(step, S0, xs)
    return jnp.moveaxis(y, 0, 1)


def rwkv7_time_mix(x, v_first, mix, w_rkv, w0, w1, w2, a0, a1, a2, g1, g2,
                   k_k, k_a, r_k, ln_g, ln_b, w_o, v_res):
    B, T, C = x.shape
    xx = token_shift(x) - x
    xm = x[None] + xx[None] * mix[:, None, None, :]
    r, k, v = jnp.einsum('pbtc,pcd->pbtd', xm[:3], w_rkv)
    xv, xw, xa, xg = xm[2], xm[3], xm[4], xm[5]
    w = -jax.nn.softplus(-(w0 + jnp.tanh(xw @ w1) @ w2)) - 0.5
    a = jax.nn.sigmoid(a0 + (xa @ a1) @ a2)
    g = jax.nn.sigmoid(xg @ g1) @ g2
    if v_res is None:
        v_first = v
    else:
        v0, v1, v2 = v_res
        v = v + (v_first - v) * jax.nn.sigmoid(v0 + (xv @ v1) @ v2)

    def heads(z):
        return z.astype(jnp.float32).reshape(B, T, RW_HEADS, RW_HEAD_DIM)

    kk = heads(k * k_k)
    kk = kk / jnp.maximum(jnp.linalg.norm(kk, axis=-1, keepdims=True), NORM_EPS)
    k = k * (1.0 + (a - 1.0) * k_a)
    rh, kh, vh, ah = heads(r), heads(k), heads(v), heads(a)
    decay = jnp.exp(-jnp.exp(heads(w)))
    y = rwkv7_recurrence(rh, decay, kh, vh, -kk, kk * ah)
    mu = jnp.mean(y, axis=-1, keepdims=True)
    var = jnp.mean(jnp.square(y - mu), axis=-1, keepdims=True)
    y = (y - mu) * lax.rsqrt(var + RW_GN_EPS)
    y = y * ln_g.astype(jnp.float32).reshape(RW_HEADS, RW_HEAD_DIM) + ln_b.astype(jnp.float32).reshape(RW_HEADS, RW_HEAD_DIM)
    y = y + jnp.sum(rh * kh * r_k.astype(jnp.float32), axis=-1, keepdims=True) * vh
    y = y.reshape(B, T, C).astype(x.dtype)
    return (y * g) @ w_o, v_first


def memory_cross_attention(x, m, wq, wkv, wo):
    B, S, _ = x.shape
    M = m.shape[1]
    q = (x @ wq).reshape(B, S, XA_HEADS, XA_HEAD_DIM)
    k, v = jnp.split(m @ wkv, 2, axis=-1)
    k = k.reshape(B, M, XA_HEADS, XA_HEAD_DIM)
    v = v.reshape(B, M, XA_HEADS, XA_HEAD_DIM)
    scores = jnp.einsum('bshd,bmhd->bhsm', q, k).astype(jnp.float32) * (XA_HEAD_DIM ** -0.5)
    p = jax.nn.softmax(scores, axis=-1).astype(x.dtype)
    o = jnp.einsum('bhsm,bmhd->bshd', p, v).reshape(B, S, D_MODEL)
    return o @ wo


def swiglu_ffn(x, w_in, w_out):
    gate, up = jnp.split(x @ w_in, 2, axis=-1)
    return (jax.nn.silu(gate) * up) @ w_out


def setup_inputs(seed: int = 0) -> dict:
    key = jax.random.key(seed)
    ks = jax.random.split(key, 48)
    ctr = [0]

    def nxt():
        ctr[0] += 1
        return ks[ctr[0] - 1]

    def nrm(shape, scale):
        return scale * jax.random.normal(nxt(), shape, jnp.float32)

    def unif(shape, lo, hi):
        return jax.random.uniform(nxt(), shape, jnp.float32, lo, hi)

    D = D_MODEL
    n_a = (DEPTH + 1) // 2
    n_b = DEPTH // 2
    n_vr = max(n_b - 1, 0)
    return {
        "x": nrm((BATCH, SEQ, D), 1.0),
        "mem": nrm((BATCH, N_MEM, D), 1.0),
        "norm_gains": 1.0 + nrm((DEPTH, 6, D), 0.02),
        "mem_norm_gains": 1.0 + nrm((DEPTH, D), 0.02),
        "xa_wq": nrm((DEPTH, D, D), D ** -0.5),
        "xa_wkv": nrm((DEPTH, D, 2 * D), D ** -0.5),
        "xa_wo": nrm((DEPTH, D, D), D ** -0.5),
        "ffn_w_in": nrm((DEPTH, D, 2 * D_FF), D ** -0.5),
        "ffn_w_out": nrm((DEPTH, D_FF, D), D_FF ** -0.5),
        "gm_w_in": nrm((n_a, D, 2 * GM_INNER), D ** -0.5),
        "gm_b_in": nrm((n_a, 2 * GM_INNER), 0.02),
        "gm_ln_g": 1.0 + nrm((n_a, GM_INNER), 0.02),
        "gm_ln_b": nrm((n_a, GM_INNER), 0.02),
        "gm_w_s": nrm((n_a, GM_GROUPS, GM_CHUNK, GM_CHUNK), GM_CHUNK ** -0.5),
        "gm_b_s": 1.0 + nrm((n_a, GM_GROUPS, GM_CHUNK), 0.02),
        "gm_w_out": nrm((n_a, GM_INNER, D), GM_INNER ** -0.5),
        "rw_mix": unif((n_b, 6, D), 0.0, 1.0),
        "rw_w_rkv": nrm((n_b, 3, D, D), D ** -0.5),
        "rw_w0": unif((n_b, D), -6.0, -1.0),
        "rw_w1": nrm((n_b, D, RW_DECAY_LORA), D ** -0.5),
        "rw_w2": nrm((n_b, RW_DECAY_LORA, D), 0.1 * RW_DECAY_LORA ** -0.5),
        "rw_a0": nrm((n_b, D), 0.1),
        "rw_a1": nrm((n_b, D, RW_AAA_LORA), D ** -0.5),
        "rw_a2": nrm((n_b, RW_AAA_LORA, D), RW_AAA_LORA ** -0.5),
        "rw_g1": nrm((n_b, D, RW_GATE_LORA), D ** -0.5),
        "rw_g2": nrm((n_b, RW_GATE_LORA, D), RW_GATE_LORA ** -0.5),
        "rw_k_k": 0.85 + nrm((n_b, D), 0.02),
        "rw_k_a": 1.0 + nrm((n_b, D), 0.02),
        "rw_r_k": nrm((n_b, RW_HEADS, RW_HEAD_DIM), 0.1),
        "rw_ln_g": 1.0 + nrm((n_b, D), 0.02),
        "rw_ln_b": nrm((n_b, D), 0.02),
        "rw_w_o": nrm((n_b, D, D), D ** -0.5),
        "rw_v0": 1.0 + nrm((n_vr, D), 0.1),
        "rw_v1": nrm((n_vr, D, RW_MV_LORA), D ** -0.5),
        "rw_v2": nrm((n_vr, RW_MV_LORA, D), RW_MV_LORA ** -0.5),
    }


def reference(x, mem, norm_gains, mem_norm_gains, xa_wq, xa_wkv, xa_wo, ffn_w_in, ffn_w_out,
              gm_w_in, gm_b_in, gm_ln_g, gm_ln_b, gm_w_s, gm_b_s, gm_w_out,
              rw_mix, rw_w_rkv, rw_w0, rw_w1, rw_w2, rw_a0, rw_a1, rw_a2, rw_g1, rw_g2,
              rw_k_k, rw_k_a, rw_r_k, rw_ln_g, rw_ln_b, rw_w_o, rw_v0, rw_v1, rw_v2):
    v_first = None
    for i in range(DEPTH):
        g = norm_gains[i]
        j = i // 2
        h = rms_norm(x, g[0])
        if i % 2 == 0:
            h = gmlp_spatial_gating(h, gm_w_in[j], gm_b_in[j], gm_ln_g[j], gm_ln_b[j],
                                    gm_w_s[j], gm_b_s[j], gm_w_out[j])
        else:
            v_res = (rw_v0[j - 1], rw_v1[j - 1], rw_v2[j - 1]) if j > 0 else None
            h, v_first = rwkv7_time_mix(h, v_first, rw_mix[j], rw_w_rkv[j], rw_w0[j], rw_w1[j], rw_w2[j],
                                        rw_a0[j], rw_a1[j], rw_a2[j], rw_g1[j], rw_g2[j],
                                        rw_k_k[j], rw_k_a[j], rw_r_k[j], rw_ln_g[j], rw_ln_b[j],
                                        rw_w_o[j], v_res)
        x = x + rms_norm(h, g[1])
        m = rms_norm(mem, mem_norm_gains[i])
        h = memory_cross_attention(rms_norm(x, g[2]), m, xa_wq[i], xa_wkv[i], xa_wo[i])
        x = x + rms_norm(h, g[3])
        h = swiglu_ffn(rms_norm(x, g[4]), ffn_w_in[i], ffn_w_out[i])
        x = x + rms_norm(h, g[5])
    return x
```

```python
import contextlib
import numpy as np
import concourse.bass as bass
import concourse.mybir as mybir
from concourse.bass_utils import run_bass_kernel_spmd

F32 = mybir.dt.float32
BF16 = mybir.dt.bfloat16
AF = mybir.ActivationFunctionType
ALU = mybir.AluOpType
AX = mybir.AxisListType

D = 1024
DFF = 2816
NMEM = 256
RMS_EPS = 1e-6
LN_EPS = 1e-5
GN_EPS = 64e-5
DEPTH = 4
NCORES = 8
CDEC = -0.6065306597126334


class Buf:
    __slots__ = ("name", "w", "r")

    def __init__(self, name):
        self.name = name
        self.w = None
        self.r = []


class T:
    __slots__ = ("ap", "bufs")

    def __init__(self, ap, bufs):
        self.ap = ap
        self.bufs = bufs if isinstance(bufs, (list, tuple)) else [bufs]

    def __getitem__(self, idx):
        return T(self.ap[idx], self.bufs)

    def v(self, ap):
        return T(ap, self.bufs)


class Sched:
    ENG = ("pe", "act", "dve", "pool", "sp")

    def __init__(self, nc, stack):
        self.nc = nc
        self.stack = stack
        self.h = {"pe": nc.tensor, "act": nc.scalar, "dve": nc.vector, "pool": nc.gpsimd, "sp": nc.sync}
        self.ops = []
        self.done = 0
        self.esem = {e: stack.enter_context(nc.semaphore("s_" + e)) for e in self.ENG}
        self.ecnt = {e: 0 for e in self.ENG}
        self.dsem = {}
        self.dcnt = {}
        self.tok = []
        self.seen = {e: {} for e in self.ENG}
        self.pending_bar = None
        self.nwait = 0
        self.ndma = 0

    def op(self, eng, fn, reads=(), writes=(), dma=None):
        rb = [b for x in reads for b in x.bufs]
        wb = [b for x in writes for b in x.bufs]
        if dma is True:
            dma = (wb[0].name if wb else "st_" + rb[0].name)
        self.ops.append((eng, fn, rb, wb, dma))
        return len(self.ops) - 1

    def _dsem(self, key):
        if key not in self.dsem:
            self.dsem[key] = self.stack.enter_context(self.nc.semaphore("d%d" % len(self.dsem)))
            self.dcnt[key] = 0
        return self.dsem[key]

    def flush(self, barrier=True):
        ops = self.ops
        lo, n = self.done, len(self.ops)
        deps = {}
        needed = set()
        for i in range(lo, n):
            eng, fn, rd, wr, dma = ops[i]
            d = set()
            for b in rd:
                if b.w is not None:
                    d.add(b.w)
            for b in wr:
                if b.w is not None:
                    d.add(b.w)
                d.update(b.r)
            for b in rd:
                b.r.append(i)
            for b in wr:
                b.w = i
                b.r = []
            dd = []
            for j in d:
                if j == i:
                    continue
                ej, _, _, _, dj = ops[j]
                if dj is None and dma is None and ej == eng and eng == "pe":
                    continue
                if dj is not None and dma is not None and dj == dma:
                    continue
                dd.append(j)
                if j >= lo:
                    needed.add(j)
            deps[i] = dd
        last = {}
        for i in range(lo, n):
            if ops[i][4] is None:
                last[ops[i][0]] = i
        if barrier:
            needed.update(last.values())
        first = {e: True for e in self.ENG}
        gfinal = {}
        for i in range(lo, n):
            k = ops[i][4]
            if isinstance(k, str) and k.startswith("G:"):
                self._dsem(k)
                gfinal[k] = gfinal.get(k, self.dcnt[k]) + 16
        for i in range(lo, n):
            eng, fn, rd, wr, dma = ops[i]
            h = self.h[eng]
            w = {}
            dl = list(deps[i])
            if first[eng] and self.pending_bar is not None:
                dl += self.pending_bar
            first[eng] = False
            for j in dl:
                t = self.tok[j]
                if t is None:
                    continue
                s, v, k = t
                if k not in w or w[k][1] < v:
                    w[k] = (s, v)
            for k, (s, v) in w.items():
                if self.seen[eng].get(k, 0) >= v:
                    continue
                h.wait_ge(s, v)
                self.nwait += 1
                self.seen[eng][k] = v
            ins = fn()
            if dma is not None:
                s = self._dsem(dma)
                self.dcnt[dma] += 16
                ins.then_inc(s, 16)
                self.tok.append((s, gfinal.get(dma, self.dcnt[dma]), "d" + str(dma)))
                self.ndma += 1
            elif i in needed:
                self.ecnt[eng] += 1
                ins.then_inc(self.esem[eng], 1)
                self.tok.append((self.esem[eng], self.ecnt[eng], "e" + eng))
            else:
                self.tok.append(None)
        if barrier:
            bar = list(last.values())
            lastd = {}
            for i in range(lo, n):
                if ops[i][4] is not None:
                    lastd[ops[i][4]] = i
            bar += list(lastd.values())
            if self.pending_bar is not None:
                for e in self.ENG:
                    if first[e]:
                        bar += [j for j in self.pending_bar]
                        break
            self.pending_bar = bar
        self.done = n

    def final_wait(self, eng):
        h = self.h[eng]
        if self.pending_bar:
            w = {}
            for j in self.pending_bar:
                t = self.tok[j]
                if t is None:
                    continue
                s, v, k = t
                if k not in w or w[k][1] < v:
                    w[k] = (s, v)
            for k, (s, v) in w.items():
                h.wait_ge(s, v)


class Builder:
    def __init__(self, TOK):
        self.TOK = TOK
        self.NT = TOK // 128
        self.nc = bass.Bass("TRN2", target_bir_lowering=False)
        self.din = {}
        self.dout = {}
        self._rr1 = 0
        self._rr2 = 0
        self.setup = True

    def inp(self, name, shape, dt=F32):
        if name not in self.din:
            self.din[name] = (self.nc.dram_tensor(name, list(shape), dt, kind="ExternalInput").ap(), tuple(shape))
        return self.din[name][0]

    def outp(self, name, shape, dt=F32):
        self.dout[name] = self.nc.dram_tensor(name, list(shape), dt, kind="ExternalOutput").ap()
        return self.dout[name]

    def sb(self, st, name, shape, dt):
        self._uid = getattr(self, "_uid", 0) + 1
        name = "%s_%d" % (name, self._uid)
        t = st.enter_context(self.nc.sbuf_tensor(name, list(shape), dt))
        return T(t[tuple(slice(None) for _ in shape)], Buf(name))

    def bank(self):
        i = self._rr1 % 8
        self._rr1 += 1
        return self.banks[i]

    def dbl(self):
        i = self._rr2 % 4
        self._rr2 += 1
        return self.dbls[i]

    def mm(self, out, lhsT, rhs, start, stop):
        nc = self.nc
        self.S.op("pe", lambda: nc.tensor.matmul(out.ap, lhsT=lhsT.ap, rhs=rhs.ap, start=start, stop=stop),
                  reads=[lhsT, rhs], writes=[out])

    def tr(self, out, in_, ident):
        nc = self.nc
        self.S.op("pe", lambda: nc.tensor.transpose(out.ap, in_.ap, ident.ap), reads=[in_, ident], writes=[out])

    def act(self, out, in_, func, bias=None, scale=None, accum=None):
        nc = self.nc
        kw = {}
        rd = [in_]
        wr = [out]
        if bias is not None:
            if isinstance(bias, T):
                kw["bias"] = bias.ap
                rd.append(bias)
            else:
                kw["bias"] = bias
        if scale is not None:
            if isinstance(scale, T):
                kw["scale"] = scale.ap
                rd.append(scale)
            else:
                kw["scale"] = scale
        if accum is not None:
            kw["accum_out"] = accum.ap
            wr.append(accum)
        self.S.op("act", lambda: nc.scalar.activation(out=out.ap, in_=in_.ap, func=func, **kw), reads=rd, writes=wr)

    def _e(self, eng):
        return self.nc.vector if eng == "dve" else self.nc.gpsimd

    def tt(self, eng, out, in0, in1, op):
        e = self._e(eng)
        self.S.op(eng, lambda: e.tensor_tensor(out=out.ap, in0=in0.ap, in1=in1.ap, op=op), reads=[in0, in1], writes=[out])

    def ts(self, eng, out, in0, s1, s2, op0, op1=None, accum=None):
        e = self._e(eng)
        rd = [in0]
        wr = [out]
        a1 = s1.ap if isinstance(s1, T) else s1
        a2 = s2.ap if isinstance(s2, T) else s2
        if isinstance(s1, T):
            rd.append(s1)
        if isinstance(s2, T):
            rd.append(s2)
        kw = {}
        if op1 is not None:
            kw["op1"] = op1
        if accum is not None:
            kw["accum_out"] = accum.ap
            wr.append(accum)
        self.S.op(eng, lambda: e.tensor_scalar(out=out.ap, in0=in0.ap, scalar1=a1, scalar2=a2, op0=op0, **kw), reads=rd, writes=wr)

    def stt(self, eng, out, in0, scalar, in1, op0, op1):
        e = self._e(eng)
        rd = [in0, in1]
        sc = scalar.ap if isinstance(scalar, T) else scalar
        if isinstance(scalar, T):
            rd.append(scalar)
        self.S.op(eng, lambda: e.scalar_tensor_tensor(out=out.ap, in0=in0.ap, scalar=sc, in1=in1.ap, op0=op0, op1=op1),
                  reads=rd, writes=[out])

    def copy(self, eng, out, in_):
        if eng == "act":
            self.act(out, in_, AF.Copy)
        else:
            e = self._e(eng)
            self.S.op(eng, lambda: e.tensor_copy(out=out.ap, in_=in_.ap), reads=[in_], writes=[out])

    def memset(self, eng, out, val):
        e = self._e(eng)
        self.S.op(eng, lambda: e.memset(out.ap, val), writes=[out])

    def reduce(self, out, in_, op):
        nc = self.nc
        self.S.op("dve", lambda: nc.vector.tensor_reduce(out=out.ap, in_=in_.ap, axis=AX.X, op=op), reads=[in_], writes=[out])

    def recip(self, out, in_):
        nc = self.nc
        self.S.op("dve", lambda: nc.vector.reciprocal(out=out.ap, in_=in_.ap), reads=[in_], writes=[out])

    def dma(self, q, out, in_):
        e = self.nc.sync if q == "sp" else self.nc.gpsimd
        key = "G:wl" if self.setup else True
        on = out.bufs[0].name
        if on.startswith("x_") or on.startswith("vfo_"):
            key = "st_" + in_.bufs[0].name
        self.S.op(q, lambda: e.dma_start(out=out.ap, in_=in_.ap), reads=[in_], writes=[out], dma=key)

    def dbg(self, name, t, shape):
        import os
        if not os.environ.get("RW_DBG"):
            return
        o = self.outp("dbg_" + name, shape)
        self.dma("pool", T(o, Buf("x_dbg_" + name)), t)

    def start(self, st):
        nc = self.nc
        self.S = Sched(nc, st)
        self.dbls = []
        self.banks = []
        for j in range(4):
            t = st.enter_context(nc.psum_tensor("psd%d" % j, [128, 1024], F32))
            b0, b1 = Buf("ps%d" % (2 * j)), Buf("ps%d" % (2 * j + 1))
            self.dbls.append(T(t[:, :], [b0, b1]))
            self.banks.append(T(t[:, 0:512], [b0]))
            self.banks.append(T(t[:, 512:1024], [b1]))
        self.ident = self.sb(st, "ident", [128, 128], BF16)
        self.ones_bf = self.sb(st, "ones_bf", [128, 128], BF16)
        self.dma("pool", self.ident, T(self.inp("c_ident", [128, 128]), Buf("c_ident")))
        self.memset("dve", self.ones_bf, 1.0)
        self.epsc = {}
        for i, e in enumerate((RMS_EPS, LN_EPS, GN_EPS)):
            self.epsc[e] = self.sb(st, "eps%d" % i, [128, 1], F32)
            self.memset("dve", self.epsc[e], e)
        self.xbufs = {}

    def xtile(self, ap2d, name, t):
        key = (name, t)
        if key not in self.xbufs:
            self.xbufs[key] = Buf("x_%s_%d" % (name, t))
        return T(ap2d[t * 128:(t + 1) * 128, :], self.xbufs[key])

    def wload(self, dst, name, shape, q="pool"):
        src = self.inp(name, shape)
        if len(shape) == 3 and shape[1] * shape[2] > 8192:
            for kc in range(shape[1]):
                self.dma(q, dst[:, kc, :], T(src[:, kc, :], Buf(name)))
        else:
            self.dma(q, dst, T(src, Buf(name)))

    def stats_rstd(self, ss, rs, eps):
        self.act(rs, ss, AF.Sqrt, bias=self.epsc[eps][:, 0:1], scale=1.0 / D)
        self.recip(rs, rs)

    def prenorm_T(self, W, xt, gcol, hT, off):
        self.act(W["junk"], xt, AF.Square, accum=W["ss"])
        self.stats_rstd(W["ss"], W["rs"], RMS_EPS)
        self.act(W["hb"], xt, AF.Copy, scale=W["rs"][:, 0:1])
        bk = self.bank()
        bkb = bk.v(bk.ap.bitcast(BF16))
        for kc in range(8):
            self.tr(bkb[:, kc * 128:(kc + 1) * 128], W["hb"][:, kc * 128:(kc + 1) * 128], self.ident)
        src = bkb.v(bkb.ap[:, 0:1024].rearrange("p (k t) -> p k t", k=8))
        self.tt("dve", hT[:, :, off:off + 128], src, gcol.v(gcol.ap.broadcast_to([128, 8, 128])), ALU.mult)

    def postnorm_res(self, W, pd, xt, GB, dst):
        self.act(W["junk"], pd, AF.Square, accum=W["ss2"])
        self.stats_rstd(W["ss2"], W["rs2"], RMS_EPS)
        self.stt("dve", W["tmp"], pd, W["rs2"][:, 0:1], GB, ALU.mult, ALU.mult)
        self.tt("pool", xt, xt, W["tmp"], ALU.add)
        self.dma("sp", dst, xt)

    def common_work(self, st, nx=4):
        W = {}
        W["junk"] = self.sb(st, "junk", [128, 1024], F32)
        W["tmp"] = self.sb(st, "tmp", [128, 1024], F32)
        W["hb"] = self.sb(st, "hb", [128, 1024], BF16)
        for n in ("ss", "rs", "ss2", "rs2"):
            W[n] = self.sb(st, n, [128, 1], F32)
        W["xt"] = [self.sb(st, "xt%d" % i, [128, 1024], F32) for i in range(nx)]
        W["hT"] = self.sb(st, "hT", [128, 8, 512], BF16)
        W["GB"] = self.sb(st, "GB", [128, 1024], F32)
        W["gcol"] = self.sb(st, "gcol", [128, 8], F32)
        return W

    def load_gains(self, W, layer, pre_idx, post_idx):
        g = self.inp("gcol%d" % layer, [128, 6, 8])
        self.dma("sp", W["gcol"], T(g[:, pre_idx, :], Buf("gcol_in")))
        gr = self.inp("grow%d" % layer, [6, 1024])
        self.dma("sp", W["GB"], T(gr[post_idx:post_idx + 1, :].broadcast_to([128, 1024]), Buf("grow_in")))

    def groups(self):
        G = min(4, self.NT)
        return [(g0, min(G, self.NT - g0)) for g0 in range(0, self.NT, G)]

    def phase_gmlp(self, layer, src, dst):
        j = layer // 2
        with contextlib.ExitStack() as st:
            W = self.common_work(st)
            win = self.sb(st, "win", [128, 8, 2048], BF16)
            wout = self.sb(st, "wout", [128, 8, 1024], BF16)
            wsr = self.sb(st, "wsr", [128, 8, 128], BF16)
            wsT = self.sb(st, "wsT", [128, 8, 128], BF16)
            masku = self.sb(st, "masku", [128, 128], BF16)
            bu = self.sb(st, "bu", [128, 8], F32)
            bv = self.sb(st, "bv", [1, 1024], BF16)
            LNG = self.sb(st, "LNG", [128, 1024], F32)
            LB2 = self.sb(st, "LB2", [1, 1024], BF16)
            R2 = self.sb(st, "R2", [1, 8, 128], BF16)
            BS2 = self.sb(st, "BS2", [1, 8, 128], BF16)
            uT = self.sb(st, "uT", [128, 8, 512], BF16)
            v32 = self.sb(st, "v32", [128, 1024], F32)
            t1 = self.sb(st, "t1", [128, 1024], F32)
            vh = self.sb(st, "vh", [128, 1024], BF16)
            gT = self.sb(st, "gT", [128, 8, 128], BF16)
            sm = {n: self.sb(st, "g_" + n, [128, 1], F32) for n in ("sv", "sv2", "mean", "m2", "var", "rstd")}
            import os
            self.load_gains(W, layer, 0, 1)
            if int(os.environ.get("GM_STOP", "9")) == -3:
                self.S.flush(); self.setup = True
                return
            self.wload(win, "gm_win%d" % j, [128, 8, 2048])
            self.wload(wout, "gm_wout%d" % j, [128, 8, 1024])
            self.wload(wsr, "gm_wsT%d" % j, [128, 8, 128])
            self.dma("pool", masku, T(self.inp("c_masku", [128, 128]), Buf("c_masku")))
            bin_ = self.inp("gm_bin%d" % j, [1, 2048])
            self.dma("sp", bu, T(self.inp("gm_bucol%d" % j, [128, 8]), Buf("bucol")))
            self.dma("pool", bv, T(bin_[0:1, 1024:2048], Buf("bin")))
            self.dma("sp", LNG, T(self.inp("gm_lng%d" % j, [1, 1024]).broadcast_to([128, 1024]), Buf("lng")))
            self.dma("pool", LB2[0:1, :], T(self.inp("gm_lnb%d" % j, [1, 1024]), Buf("lnb")))

            if int(os.environ.get("GM_STOP", "9")) == -2:
                self.S.flush(); self.setup = True
                return
            for g in range(8):
                self.tt("dve", wsT[:, g, :], wsr[:, g, :], masku, ALU.mult)
            bkA = self.bank()
            for g in range(4):
                self.mm(bkA[0:1, g * 128:(g + 1) * 128], self.ones_bf[:, 0:1], wsT[:, g, :], True, True)
            self.copy("dve", R2[0:1, 0:4, :], bkA.v(bkA.ap[0:1, 0:512].rearrange("p (g t) -> p g t", g=4)))
            bkB = self.bank()
            for g in range(4):
                self.mm(bkB[0:1, g * 128:(g + 1) * 128], self.ones_bf[:, 0:1], wsT[:, 4 + g, :], True, True)
            self.copy("dve", R2[0:1, 4:8, :], bkB.v(bkB.ap[0:1, 0:512].rearrange("p (g t) -> p g t", g=4)))

            if int(os.environ.get("GM_STOP", "9")) == -1:
                self.S.flush(); self.setup = True
                return
            bsr = self.inp("gm_bs%d" % j, [1, 8, 128])
            self.dma("pool", BS2, T(bsr, Buf("bs")))
            self.setup = False
            import os
            GS = int(os.environ.get("GM_STOP", "9"))
            if GS <= 0:
                self.S.flush()
                self.setup = True
                return

            for (g0, gn) in self.groups():
                ntok = gn * 128
                for t in range(gn):
                    self.dma("sp", W["xt"][t], self.xtile(src[1], src[0], g0 + t))
                    self.prenorm_T(W, W["xt"][t], W["gcol"], W["hT"], t * 128)
                for n in range(8):
                    bk = self.bank()
                    for kc in range(8):
                        self.mm(bk[:, 0:ntok], win[:, kc, n * 128:(n + 1) * 128], W["hT"][:, kc, 0:ntok], kc == 0, kc == 7)
                    self.act(uT[:, n, 0:ntok], bk[:, 0:ntok], AF.Gelu, bias=bu[:, n:n + 1])
                for t in range(gn):
                    pd = self.dbl()
                    for half in range(2):
                        o = pd[:, half * 512:(half + 1) * 512]
                        for kc in range(8):
                            self.mm(o, W["hT"][:, kc, t * 128:(t + 1) * 128], win[:, kc, 1024 + half * 512:1024 + (half + 1) * 512], kc == 0, False)
                        self.mm(o, self.ones_bf[0:1, 0:128], bv[0:1, half * 512:(half + 1) * 512], False, True)
                    self.act(v32, pd, AF.Gelu, accum=sm["sv"])
                    self.act(W["junk"], v32, AF.Square, accum=sm["sv2"])
                    self.ts("dve", sm["mean"], sm["sv"], 1.0 / D, None, ALU.mult)
                    self.tt("dve", sm["m2"], sm["mean"], sm["mean"], ALU.mult)
                    self.stt("dve", sm["var"], sm["sv2"], 1.0 / D, sm["m2"], ALU.mult, ALU.subtract)
                    self.act(sm["rstd"], sm["var"], AF.Sqrt, bias=self.epsc[LN_EPS][:, 0:1], scale=1.0)
                    self.recip(sm["rstd"], sm["rstd"])
                    self.stt("dve", t1, v32, sm["mean"][:, 0:1], LNG, ALU.subtract, ALU.mult)
                    self.act(vh, t1, AF.Copy, scale=sm["rstd"][:, 0:1])
                    for half in range(2):
                        bk = self.bank()
                        for gi in range(4):
                            g = half * 4 + gi
                            o = bk[:, gi * 128:(gi + 1) * 128]
                            self.mm(o, vh[:, g * 128:(g + 1) * 128], wsT[:, g, :], True, False)
                            self.mm(o, LB2[0:1, g * 128:(g + 1) * 128], R2[0:1, g, :], False, False)
                            self.mm(o, self.ones_bf[0:1, 0:128], BS2[0:1, g, :], False, True)
                        self.tt("dve", gT[:, half * 4:(half + 1) * 4, :], bk.v(bk.ap.rearrange("p (g t) -> p g t", g=4)),
                                uT[:, half * 4:(half + 1) * 4, t * 128:(t + 1) * 128], ALU.mult)
                    pd2 = self.dbl()
                    for half in range(2):
                        o = pd2[:, half * 512:(half + 1) * 512]
                        for kc in range(8):
                            self.mm(o, gT[:, kc, :], wout[:, kc, half * 512:(half + 1) * 512], kc == 0, kc == 7)
                    self.postnorm_res(W, pd2, W["xt"][t], W["GB"], self.xtile(dst[1], dst[0], g0 + t))
            self.S.flush()
            self.setup = True

    def phase_xattn(self, layer, src, dst):
        with contextlib.ExitStack() as st:
            W = self.common_work(st)
            wq = self.sb(st, "wq", [128, 8, 1024], BF16)
            wo = self.sb(st, "wo", [128, 8, 1024], BF16)
            kT = self.sb(st, "kT", [128, 8, 256], BF16)
            vtm = self.sb(st, "vtm", [128, 2, 1024], BF16)
            qT = self.sb(st, "qT", [128, 8, 512], BF16)
            pT = self.sb(st, "pT", [128, 8, 512], BF16)
            oT = self.sb(st, "oT", [128, 8, 512], BF16)
            pe32 = self.sb(st, "pe32", [128, 1024], F32)
            pb16 = self.sb(st, "pb16", [128, 1024], BF16)
            mx = self.sb(st, "mx", [128, 4], F32)
            nmx = self.sb(st, "nmx", [128, 4], F32)
            smx = self.sb(st, "smx", [128, 4], F32)
            rsm = self.sb(st, "rsm", [128, 4], F32)
            self.wload(wq, "xa_wq%d" % layer, [128, 8, 1024])
            self.wload(wo, "xa_wo%d" % layer, [128, 8, 1024])
            with contextlib.ExitStack() as st2:
                wkv = self.sb(st2, "wkv", [128, 8, 2048], BF16)
                mT = self.sb(st2, "mT", [128, 8, 256], BF16)
                mg = self.sb(st2, "mg", [128, 8], F32)
                self.wload(wkv, "xa_wkv%d" % layer, [128, 8, 2048])
                self.dma("sp", mg, T(self.inp("memg%d" % layer, [128, 8]), Buf("memg")))
                mem = self.inp("mem", [NMEM, D])
                for mt in range(2):
                    self.dma("sp", W["xt"][mt], T(mem[mt * 128:(mt + 1) * 128, :], Buf("mem_in")))
                    self.prenorm_T(W, W["xt"][mt], mg, mT, mt * 128)
                for n in range(8):
                    bk = self.bank()
                    for kc in range(8):
                        self.mm(bk[:, 0:256], wkv[:, kc, n * 128:(n + 1) * 128], mT[:, kc, :], kc == 0, kc == 7)
                    self.copy("act", kT[:, n, :], bk[:, 0:256])
                for mt in range(2):
                    pd = self.dbl()
                    for half in range(2):
                        o = pd[:, half * 512:(half + 1) * 512]
                        for kc in range(8):
                            self.mm(o, mT[:, kc, mt * 128:(mt + 1) * 128], wkv[:, kc, 1024 + half * 512:1024 + (half + 1) * 512], kc == 0, kc == 7)
                    self.copy("act", vtm[:, mt, :], pd)
                self.S.flush()
            self.load_gains(W, layer, 2, 3)
            self.setup = False
            for (g0, gn) in self.groups():
                ntok = gn * 128
                for t in range(gn):
                    self.dma("sp", W["xt"][t], self.xtile(src[1], src[0], g0 + t))
                    self.prenorm_T(W, W["xt"][t], W["gcol"], W["hT"], t * 128)
                for n in range(8):
                    bk = self.bank()
                    for kc in range(8):
                        self.mm(bk[:, 0:ntok], wq[:, kc, n * 128:(n + 1) * 128], W["hT"][:, kc, 0:ntok], kc == 0, kc == 7)
                    self.copy("act", qT[:, n, 0:ntok], bk[:, 0:ntok])
                import os
                XS = int(os.environ.get("XA_STOP", "9"))
                if XS <= 1:
                    continue
                for t in range(gn):
                    pd = self.dbl()
                    for h in range(4):
                        for half in range(2):
                            self.mm(pd[:, h * 256:(h + 1) * 256], qT[:, 2 * h + half, t * 128:(t + 1) * 128], kT[:, 2 * h + half, :], half == 0, half == 1)
                    self.reduce(mx, pd.v(pd.ap.rearrange("p (h m) -> p h m", h=4)), ALU.max)
                    self.ts("dve", nmx, mx, -1.0 / 16.0, None, ALU.mult)
                    for h in range(4):
                        self.act(pe32[:, h * 256:(h + 1) * 256], pd[:, h * 256:(h + 1) * 256], AF.Exp, bias=nmx[:, h:h + 1],
                                 scale=1.0 / 16.0, accum=smx[:, h:h + 1])
                    if XS <= 2:
                        continue
                    self.recip(rsm, smx)
                    self.tt("dve", pb16.v(pb16.ap.rearrange("p (h m) -> p h m", h=4)), pe32.v(pe32.ap.rearrange("p (h m) -> p h m", h=4)),
                            rsm.v(rsm.ap.broadcast_to([128, 4, 256])), ALU.mult)
                    if XS <= 3:
                        continue
                    bk = self.bank()
                    bkb = bk.v(bk.ap.bitcast(BF16))
                    for c in range(8):
                        self.tr(bkb[:, c * 128:(c + 1) * 128], pb16[:, c * 128:(c + 1) * 128], self.ident)
                    self.copy("act", pT[:, :, t * 128:(t + 1) * 128], bkb.v(bkb.ap[:, 0:1024].rearrange("p (c s) -> p c s", c=8)))
                if XS <= 4:
                    continue
                for n in range(8):
                    h, dh = n // 2, n % 2
                    bk = self.bank()
                    for mh in range(2):
                        self.mm(bk[:, 0:ntok], vtm[:, mh, h * 256 + dh * 128:h * 256 + (dh + 1) * 128], pT[:, 2 * h + mh, 0:ntok], mh == 0, mh == 1)
                    self.copy("act", oT[:, n, 0:ntok], bk[:, 0:ntok])
                if XS <= 5:
                    continue
                for t in range(gn):
                    pd2 = self.dbl()
                    for half in range(2):
                        o = pd2[:, half * 512:(half + 1) * 512]
                        for kc in range(8):
                            self.mm(o, oT[:, kc, t * 128:(t + 1) * 128], wo[:, kc, half * 512:(half + 1) * 512], kc == 0, kc == 7)
                    self.postnorm_res(W, pd2, W["xt"][t], W["GB"], self.xtile(dst[1], dst[0], g0 + t))
            self.S.flush()
            self.setup = True

    def phase_ffn(self, layer, src, dst):
        NH = DFF // 128
        with contextlib.ExitStack() as st:
            W = self.common_work(st)
            win = self.sb(st, "fwin", [128, 8, 2 * DFF], BF16)
            wout = self.sb(st, "fwout", [128, NH, 1024], BF16)
            hid = self.sb(st, "hid", [128, NH, 512], BF16)
            sg = self.sb(st, "sg", [128, 512], BF16)
            self.load_gains(W, layer, 4, 5)
            self.wload(win, "ffn_win%d" % layer, [128, 8, 2 * DFF])
            self.wload(wout, "ffn_wout%d" % layer, [128, NH, 1024])
            self.setup = False
            for (g0, gn) in self.groups():
                ntok = gn * 128
                for t in range(gn):
                    self.dma("sp", W["xt"][t], self.xtile(src[1], src[0], g0 + t))
                    self.prenorm_T(W, W["xt"][t], W["gcol"], W["hT"], t * 128)
                for n in range(NH):
                    bg = self.bank()
                    bu = self.bank()
                    for kc in range(8):
                        self.mm(bg[:, 0:ntok], win[:, kc, n * 128:(n + 1) * 128], W["hT"][:, kc, 0:ntok], kc == 0, kc == 7)
                    for kc in range(8):
                        self.mm(bu[:, 0:ntok], win[:, kc, DFF + n * 128:DFF + (n + 1) * 128], W["hT"][:, kc, 0:ntok], kc == 0, kc == 7)
                    self.act(sg[:, 0:ntok], bg[:, 0:ntok], AF.Silu)
                    self.tt("dve", hid[:, n, 0:ntok], sg[:, 0:ntok], bu[:, 0:ntok], ALU.mult)
                for t in range(gn):
                    pd2 = self.dbl()
                    for half in range(2):
                        o = pd2[:, half * 512:(half + 1) * 512]
                        for kc in range(NH):
                            self.mm(o, hid[:, kc, t * 128:(t + 1) * 128], wout[:, kc, half * 512:(half + 1) * 512], kc == 0, kc == NH - 1)
                    self.postnorm_res(W, pd2, W["xt"][t], W["GB"], self.xtile(dst[1], dst[0], g0 + t))
            self.S.flush()
            self.setup = True


    def phase_rwkv(self, layer, dst, seq=True):
        j = layer // 2
        has_vres = j > 0
        TOK, NT = self.TOK, self.NT
        if seq:
            NT4 = NT
            xin4 = self.inp("xin", [TOK, D])
            xprev = self.inp("xprev", [128, D])
            st_in = self.inp("st_in", [64, 16, 64])
            st_out = self.outp("st_out", [64, 16, 64])
            vf4 = self.inp("vf_in", [TOK, D]) if has_vres else None
        else:
            NT4 = 4 * NT
            xin4 = self.inp("xin4", [4 * TOK, D])
            vf4 = self.inp("vf4", [4 * TOK, D]) if has_vres else None
        vfo = self.outp("vf_out", [TOK, D]) if not has_vres else None
        with contextlib.ExitStack() as st:
            sb = lambda n, sh, dt: self.sb(st, n, sh, dt)
            W = {}
            W["junk"] = sb("junk", [128, 1024], F32)
            W["tmp"] = sb("tmp", [128, 1024], F32)
            for n in ("ss", "rs", "ss2", "rs2"):
                W[n] = sb(n, [128, 1], F32)
            W["GB"] = sb("GB", [128, 1024], F32)
            W["gcol"] = sb("gcol", [128, 8], F32)
            xt = sb("xt", [128, 1024], F32)
            self.load_gains(W, layer, 0, 1)
            import os
            wr = sb("wr", [128, 8, 1024], BF16)
            if os.environ.get("RW_ALIAS"):
                wk = wr; wv = wr; wo = wr
            else:
                wk = sb("wk", [128, 8, 1024], BF16); wv = sb("wv", [128, 8, 1024], BF16); wo = sb("wo", [128, 8, 1024], BF16)
            self.wload(wr, "rw_wr%d" % j, [128, 8, 1024]); self.wload(wk, "rw_wk%d" % j, [128, 8, 1024])
            self.wload(wv, "rw_wv%d" % j, [128, 8, 1024]); self.wload(wo, "rw_wo%d" % j, [128, 8, 1024])
            w1 = sb("w1", [128, 8, 64], BF16); a1 = sb("a1", [128, 8, 64], BF16); g1 = sb("g1", [128, 8, 160], BF16)
            self.wload(w1, "rw_w1%d" % j, [128, 8, 64]); self.wload(a1, "rw_a1%d" % j, [128, 8, 64]); self.wload(g1, "rw_g1%d" % j, [128, 8, 160])
            w2 = sb("w2", [64, 1024], BF16); a2 = sb("a2", [64, 1024], BF16)
            g2a = sb("g2a", [128, 1024], BF16); g2b = sb("g2b", [32, 1024], BF16)
            self.wload(w2, "rw_w2%d" % j, [64, 1024]); self.wload(a2, "rw_a2%d" % j, [64, 1024])
            g2d = self.inp("rw_g2%d" % j, [160, 1024])
            self.dma("pool", g2a, T(g2d[0:128, :], Buf("g2d"))); self.dma("pool", g2b, T(g2d[128:160, :], Buf("g2d2")))
            a0r = sb("a0r", [1, 1024], BF16)
            self.wload(a0r, "rw_a0%d" % j, [1, 1024])
            W0B = sb("W0B", [128, 1024], F32)
            self.dma("sp", W0B, T(self.inp("rw_w0%d" % j, [1, 1024]).broadcast_to([128, 1024]), Buf("w0in")))
            if has_vres:
                v1 = sb("v1", [128, 8, 32], BF16); v2 = sb("v2", [32, 1024], BF16); v0r = sb("v0r", [1, 1024], BF16)
                self.wload(v1, "rw_v1", [128, 8, 32]); self.wload(v2, "rw_v2", [32, 1024]); self.wload(v0r, "rw_v0", [1, 1024])
            mixc = sb("mixc", [128, 6, 8], F32)
            self.dma("sp", mixc, T(self.inp("rw_mixcol%d" % j, [128, 6, 8]), Buf("mixcol")))
            cb = {}
            for n in ("kk", "ka", "rk", "lng", "lnb"):
                lowp = n in ("rk", "lng", "lnb")
                cb[n] = sb("cb_" + n, [128, 1024], BF16 if lowp else F32)
                self.dma("pool" if lowp else "sp", cb[n], T(self.inp("rw_%s%d" % (n, j), [1, 1024]).broadcast_to([128, 1024]), Buf("cbin" + n)))
            msk = {}
            for n in ("su4", "iu4", "sl4"):
                m_ = sb("m_" + n, [128, 128], BF16)
                self.dma("pool", m_, T(self.inp("c_mask_" + n, [128, 128]), Buf("cm" + n)))
                msk[n] = m_.v(m_.ap[:, None, :].broadcast_to([128, 4, 128]))
            tri = {}
            for n in ("incl", "excl", "rem"):
                tri[n] = sb("tri_" + n, [128, 128], BF16)
                self.dma("pool", tri[n], T(self.inp("c_tri_" + n, [128, 128]), Buf("ct" + n)))
            hTc = sb("hTc", [128, 8, 129], BF16)
            XX_ALIAS = True
            r32 = sb("r32", [128, 1024], F32); k32 = sb("k32", [128, 1024], F32); v32 = sb("v32", [128, 1024], F32)
            sg = sb("sg", [128, 1024], F32); a32 = sb("a32", [128, 1024], F32); kkn = sb("kkn", [128, 1024], F32)
            b32 = a32; pex = sg; y32 = W["tmp"]
            hi = sb("hi", [128, 1024], BF16); lo = sb("lo", [128, 1024], BF16); W["hb"] = lo; gbf = sb("gbf", [128, 1024], BF16)
            vbf = sb("vbf", [128, 1024], BF16)
            til = {n: sb("t_" + n, [128, 1024], BF16) for n in ("A", "R", "B", "K", "Be", "Ke")}
            v8 = lambda t_: t_.v(t_.ap.rearrange("p (k t) -> p k t", k=8))
            xx = v8(til["Ke"])
            xm = [v8(til["B"]), v8(til["K"])]
            ygb = hi; ygT = xm[0]
            tT = {n: sb("tT_" + n, [64, 16, 128], BF16) for n in ("A", "R", "B", "K")}
            l1w = sb("l1w", [64, 128], BF16); l1a = sb("l1a", [64, 128], BF16); l1g = sb("l1g", [128, 128], BF16)
            l1g2 = sb("l1g2", [32, 128], BF16); l1v = sb("l1v", [32, 128], BF16)
            st16 = {n: sb("s16_" + n, [128, 16], F32) for n in ("ss", "rn", "s1", "s2", "mean", "m2", "var", "rstd", "bon")}
            LT = [sb("LT0", [128, 4, 128], BF16)]
            Lm = [sb("Lm0", [128, 4, 128], BF16)]
            hb_ = {n: sb("hb_" + n, [128, 4, 128], BF16) for n in ("X", "XT", "X2", "X2T", "T", "TT", "A1", "OkT")}
            hm = {}
            for n in ("d8", "o16", "o32", "o64", "o128"):
                hm[n] = sb("hm_" + n, [128, 128], BF16)
                self.dma("pool", hm[n], T(self.inp("c_hm_" + n, [128, 128]), Buf("chm" + n)))
            MrbT = sb("MrbT", [128, 4, 128], BF16); LakT = sb("LakT", [128, 4, 128], BF16); MrkT = sb("MrkT", [128, 4, 128], BF16)
            Z = sb("Z", [128, 4, 128], BF16)
            Tcw = sb("Tcw", [64, 4, 64], BF16); RmT = sb("RmT", [64, 4, 128], BF16)
            ST = sb("ST", [64, 16, 64], F32); STb = sb("STb", [64, 16, 64], BF16)
            PCc = sb("PCc", [64, 16], F32)
            if seq:
                self.dma("sp", ST, T(st_in, Buf("st_in")))
                self.copy("dve", STb, ST)
            else:
                self.memset("dve", ST, 0.0)
                self.memset("dve", STb, 0.0)
            self.memset("dve", hTc, 0.0)

            def v3(t, n=16):
                return t.v(t.ap.rearrange("p (h j) -> p h j", h=n))

            def bc16(t):
                return t.v(t.ap.broadcast_to([128, 16, 64]))

            def proj_tm(xmT, w, nK=8):
                pd = self.dbl()
                for half in range(2):
                    o = pd[:, half * 512:(half + 1) * 512]
                    for kc in range(nK):
                        self.mm(o, xmT[:, kc, :], w[:, kc, half * 512:(half + 1) * 512], kc == 0, kc == nK - 1)
                return pd

            def mix(p, slot):
                eng = "dve"
                o = xm[slot]
                self.tt(eng, o, xx, mixc.v(mixc.ap[:, p, :].broadcast_to([128, 8, 128])), ALU.mult)
                self.tt(eng, o, o, hTc[:, :, 1:129], ALU.add)
                return o

            def lora1(xmT, wl, n, dstT, func):
                bk = self.bank()
                for kc in range(8):
                    self.mm(bk[0:n, 0:128], wl[:, kc, 0:n] if n <= 128 else None, xmT[:, kc, :], kc == 0, kc == 7)
                self.act(dstT, bk[0:n, 0:128], func)

            self.setup = False
            if seq:
                self.dma("sp", xt, T(xprev, Buf("xprev_in")))
                self.prenorm_T(W, xt, W["gcol"], hTc, 1)
            for ti in range(NT4):
                own = True if seq else ti >= 3 * NT
                to = ti if seq else ti - 3 * NT
                self.dma("sp", xt, T(xin4[ti * 128:(ti + 1) * 128, :], Buf("xin4_%d" % ti)))
                self.copy("dve", hTc[:, :, 0:1], hTc[:, :, 128:129])
                self.prenorm_T(W, xt, W["gcol"], hTc, 1)
                self.tt("dve", xx, hTc[:, :, 0:128], hTc[:, :, 1:129], ALU.subtract)
                STOP = int(os.environ.get("RW_STOP", "9"))
                if STOP <= 1:
                    continue
                m = mix(0, 0)
                self.copy("act", r32, proj_tm(m, wr))
                m = mix(1, 1)
                self.copy("act", k32, proj_tm(m, wk))
                m = mix(2, 0)
                self.copy("act", v32, proj_tm(m, wv))
                if has_vres:
                    lora1(m, v1, 32, l1v, AF.Copy)
                    pd = self.dbl()
                    for half in range(2):
                        o = pd[:, half * 512:(half + 1) * 512]
                        self.mm(o, l1v[0:32, :], v2[0:32, half * 512:(half + 1) * 512], True, False)
                        self.mm(o, self.ones_bf[0:1, 0:128], v0r[0:1, half * 512:(half + 1) * 512], False, True)
                    self.act(pex, pd, AF.Sigmoid)
                    self.dma("sp", W["tmp"], T(vf4[ti * 128:(ti + 1) * 128, :], Buf("vf4_%d" % ti)))
                    self.tt("dve", W["tmp"], W["tmp"], v32, ALU.subtract)
                    self.tt("dve", W["tmp"], W["tmp"], pex, ALU.mult)
                    self.tt("dve", v32, v32, W["tmp"], ALU.add)
                elif own:
                    self.dma("sp", T(vfo[to * 128:(to + 1) * 128, :], Buf("vfo_%d" % to)), v32)
                self.copy("pool", vbf, v32)
                m = mix(3, 1)
                lora1(m, w1, 64, l1w, AF.Tanh)
                pd = self.dbl()
                for half in range(2):
                    o = pd[:, half * 512:(half + 1) * 512]
                    self.mm(o, l1w[0:64, :], w2[0:64, half * 512:(half + 1) * 512], True, True)
                self.tt("dve", W["junk"], pd, W0B, ALU.add)
                self.act(sg, W["junk"], AF.Sigmoid)
                m = mix(4, 0)
                lora1(m, a1, 64, l1a, AF.Copy)
                pd = self.dbl()
                for half in range(2):
                    o = pd[:, half * 512:(half + 1) * 512]
                    self.mm(o, l1a[0:64, :], a2[0:64, half * 512:(half + 1) * 512], True, False)
                    self.mm(o, self.ones_bf[0:1, 0:128], a0r[0:1, half * 512:(half + 1) * 512], False, True)
                self.act(a32, pd, AF.Sigmoid)
                if own:
                    m = mix(5, 1)
                    lora1(m, g1, 128, l1g, AF.Sigmoid)
                    bk = self.bank()
                    for kc in range(8):
                        self.mm(bk[0:32, 0:128], g1[:, kc, 128:160], m[:, kc, :], kc == 0, kc == 7)
                    self.act(l1g2, bk[0:32, 0:128], AF.Sigmoid)
                    pd = self.dbl()
                    for half in range(2):
                        o = pd[:, half * 512:(half + 1) * 512]
                        self.mm(o, l1g, g2a[:, half * 512:(half + 1) * 512], True, False)
                        self.mm(o, l1g2[0:32, :], g2b[0:32, half * 512:(half + 1) * 512], False, True)
                    self.copy("act", gbf, pd)
                if STOP <= 2:
                    continue
                self.tt("dve", kkn, k32, cb["kk"], ALU.mult)
                self.tt("pool", W["junk"], kkn, kkn, ALU.mult)
                self.reduce(st16["ss"], v3(W["junk"]), ALU.add)
                self.ts("dve", st16["ss"], st16["ss"], 1e-24, None, ALU.max)
                self.act(st16["rn"], st16["ss"], AF.Sqrt)
                self.recip(st16["rn"], st16["rn"])
                self.tt("dve", v3(kkn), v3(kkn), bc16(st16["rn"]), ALU.mult)
                self.stt("dve", W["junk"], a32, -1.0, cb["ka"], ALU.add, ALU.mult)
                self.stt("dve", k32, W["junk"], 1.0, k32, ALU.add, ALU.mult)
                self.tt("dve", b32, kkn, a32, ALU.mult)
                if own and to == 0:
                    self.dbg("r", r32, [128, 1024]); self.dbg("kp", k32, [128, 1024]); self.dbg("v", v32, [128, 1024])
                    self.dbg("sg", sg, [128, 1024]); self.dbg("b", b32, [128, 1024]); self.dbg("kkn", kkn, [128, 1024])
                self.copy("dve", hi, sg)
                self.tt("dve", lo, sg, hi, ALU.subtract)
                def cum(which):
                    pd = self.dbl()
                    for half in range(2):
                        o = pd[:, half * 512:(half + 1) * 512]
                        self.mm(o, tri[which], hi[:, half * 512:(half + 1) * 512], True, False)
                        self.mm(o, tri[which], lo[:, half * 512:(half + 1) * 512], False, True)
                    return pd
                pd = cum("excl")
                self.act(pex, pd, AF.Exp, scale=CDEC)
                self.stt("dve", til["A"], kkn, -1.0, pex, ALU.mult, ALU.mult)
                pd = cum("incl")
                self.act(pex, pd, AF.Exp, scale=CDEC)
                if own and to == 0:
                    self.dbg("Pincl", pex, [128, 1024])
                self.tt("dve", til["R"], r32, pex, ALU.mult)
                self.recip(pex, pex)
                self.tt("dve", til["B"], b32, pex, ALU.mult)
                self.tt("pool", til["K"], k32, pex, ALU.mult)
                pd = cum("rem")
                self.act(pex, pd, AF.Exp, scale=CDEC)
                self.tt("dve", til["Be"], b32, pex, ALU.mult)
                self.tt("pool", til["Ke"], k32, pex, ALU.mult)
                if own and to == 0:
                    for n_ in ("A", "R", "B", "K", "Be", "Ke"):
                        self.dbg("t" + n_, til[n_], [128, 1024])
                bk = self.bank()
                for h in range(16):
                    self.mm(bk[0:64, h:h + 1], hi[:, h * 64:(h + 1) * 64], self.ones_bf[:, 0:1], True, False)
                    self.mm(bk[0:64, h:h + 1], lo[:, h * 64:(h + 1) * 64], self.ones_bf[:, 0:1], False, True)
                self.act(PCc, bk[0:64, 0:16], AF.Exp, scale=CDEC)
                if STOP <= 3:
                    continue
                for n in ("A", "R", "B", "K"):
                    for hh in range(2):
                        bk = self.bank()
                        bkb = bk.v(bk.ap.bitcast(BF16))
                        for c in range(8):
                            h = hh * 8 + c
                            self.tr(bkb[0:64, c * 128:(c + 1) * 128], til[n][:, h * 64:(h + 1) * 64], self.ident)
                        self.copy("act", tT[n][:, hh * 8:(hh + 1) * 8, :], bkb.v(bkb.ap[0:64, 0:1024].rearrange("p (c s) -> p c s", c=8)))
                if STOP <= 4:
                    continue
                for hg in range(4):
                    bLT, bL, bRB, bAK, bRK = self.bank(), self.bank(), self.bank(), self.bank(), self.bank()
                    for hi_ in range(4):
                        h = hg * 4 + hi_
                        pr, base = h // 2, ((h % 2) * 64 if not os.environ.get("RW_BASE0") else 0)
                        At = tT["A"][0:64, h, :]; Rt = tT["R"][0:64, h, :]
                        Bt = tT["B"][0:64, h, :]; Kt = tT["K"][0:64, h, :]
                        sl = slice(hi_ * 128, (hi_ + 1) * 128)
                        self.mm(bLT[:, sl], Bt, At, True, True)
                        self.mm(bL[:, sl], At, Bt, True, True)
                        self.mm(bRB[:, sl], Bt, Rt, True, True)
                        self.mm(bAK[:, sl], Kt, At, True, True)
                        self.mm(bRK[:, sl], Kt, Rt, True, True)
                    g4 = lambda b: b.v(b.ap.rearrange("p (h t) -> p h t", h=4))
                    self.tt("dve", LT[0], g4(bLT), msk["su4"], ALU.mult)
                    self.tt("dve", Lm[0], g4(bL), msk["sl4"], ALU.mult)
                    self.tt("dve", MrbT, g4(bRB), msk["iu4"], ALU.mult)
                    self.tt("dve", LakT, g4(bAK), msk["su4"], ALU.mult)
                    self.tt("dve", MrkT, g4(bRK), msk["iu4"], ALU.mult)
                    if own and to == 0 and hg == 0:
                        self.dbg("LT0", LT[0], [128, 4, 128]); self.dbg("Lm0", Lm[0], [128, 4, 128]); self.dbg("MrbT", MrbT, [128, 4, 128])
                        self.dbg("LakT", LakT, [128, 4, 128]); self.dbg("MrkT", MrkT, [128, 4, 128])
                        self.dbg("tTA", tT["A"], [64, 16, 128]); self.dbg("tTK", tT["K"], [64, 16, 128])
                    bX = self.bank()
                    for hi_ in range(4):
                        h = hg * 4 + hi_
                        self.mm(bX[:, hi_ * 64:(hi_ + 1) * 64], LakT[:, hi_, :], vbf[:, h * 64:(h + 1) * 64], True, True)
                    self.copy("pool", Z[:, :, 0:64], til["A"].v(til["A"].ap[:, hg * 256:(hg + 1) * 256].rearrange("p (h j) -> p h j", h=4)))
                    self.copy("act", Z[:, :, 64:128], bX.v(bX.ap[:, 0:256].rearrange("p (h j) -> p h j", h=4)))
                    if own and to == 0 and hg == 0:
                        self.dbg("Z0", Z, [128, 4, 128])
                    LTf, Lf = LT[0], Lm[0]
                    bc4 = lambda m_: m_.v(m_.ap[:, None, :].broadcast_to([128, 4, 128]))
                    X, XT, X2, X2T, Tm, TT, A1, OkT = hb_["X"], hb_["XT"], hb_["X2"], hb_["X2T"], hb_["T"], hb_["TT"], hb_["A1"], hb_["OkT"]

                    def mm4(bank_, l_, r_):
                        for hi_ in range(4):
                            self.mm(bank_[:, hi_ * 128:(hi_ + 1) * 128], l_[:, hi_, :], r_[:, hi_, :], True, True)

                    self.tt("dve", X, Lf, bc4(hm["d8"]), ALU.mult)
                    self.tt("pool", XT, LTf, bc4(hm["d8"]), ALU.mult)
                    self.tt("dve", Tm, X, bc4(self.ident), ALU.add)
                    self.tt("pool", TT, XT, bc4(self.ident), ALU.add)
                    for rep in range(2):
                        src, srcT = (X, XT) if rep == 0 else (X2, X2T)
                        dstm, dstT = (X2, X2T) if rep == 0 else (X, XT)
                        b2, b2T = self.bank(), self.bank()
                        mm4(b2, srcT, src)
                        mm4(b2T, src, srcT)
                        self.copy("act", dstm, g4(b2))
                        self.copy("act", dstT, g4(b2T))
                        bA, bB = self.bank(), self.bank()
                        mm4(bA, dstT, Tm)
                        mm4(bB, dstm, TT)
                        self.tt("dve", Tm, g4(bA), Tm, ALU.add)
                        self.tt("dve", TT, g4(bB), TT, ALU.add)
                    for kk_ in ("o16", "o32", "o64", "o128"):
                        self.tt("pool", OkT, LTf, bc4(hm[kk_]), ALU.mult)
                        bA = self.bank()
                        mm4(bA, OkT, Tm)
                        self.copy("act", A1, g4(bA))
                        bB, bC = self.bank(), self.bank()
                        mm4(bB, TT, A1)
                        mm4(bC, A1, TT)
                        self.tt("dve", Tm, g4(bB), Tm, ALU.add)
                        self.tt("dve", TT, g4(bC), TT, ALU.add)
                    bZ = self.bank()
                    mm4(bZ, TT, Z)
                    self.copy("act", Z, g4(bZ))
                    if own and to == 0 and hg == 0:
                        self.dbg("Zfin", Z, [128, 4, 128])
                    bT = self.bank()
                    for hi_ in range(4):
                        h = hg * 4 + hi_
                        self.mm(bT[0:64, hi_ * 64:(hi_ + 1) * 64], Z[:, hi_, 0:64], til["Be"][:, h * 64:(h + 1) * 64], True, True)
                    self.copy("act", Tcw, bT.v(bT.ap[0:64, 0:256].rearrange("p (h j) -> p h j", h=4)))
                    if own:
                        bR = self.bank()
                        for hi_ in range(4):
                            h = hg * 4 + hi_
                            sl = slice(hi_ * 128, (hi_ + 1) * 128)
                            self.mm(bR[0:64, sl], Z[:, hi_, 0:64], MrbT[:, hi_, :], True, False)
                            self.mm(bR[0:64, sl], til["R"][:, h * 64:(h + 1) * 64], self.ident, False, True)
                        self.copy("act", RmT, bR.v(bR.ap[0:64, :].rearrange("p (h t) -> p h t", h=4)))
                        bY = self.bank()
                        for hi_ in range(4):
                            h = hg * 4 + hi_
                            o = bY[:, hi_ * 64:(hi_ + 1) * 64]
                            self.mm(o, MrbT[:, hi_, :], Z[:, hi_, 64:128], True, False)
                            self.mm(o, MrkT[:, hi_, :], vbf[:, h * 64:(h + 1) * 64], False, False)
                            self.mm(o, RmT[0:64, hi_, :], STb[0:64, h, :], False, True)
                        self.copy("act", y32[:, hg * 256:(hg + 1) * 256], bY[:, 0:256])
                    bG = self.bank()
                    for hi_ in range(4):
                        h = hg * 4 + hi_
                        o = bG[0:64, hi_ * 64:(hi_ + 1) * 64]
                        self.mm(o, til["Be"][:, h * 64:(h + 1) * 64], Z[:, hi_, 64:128], True, False)
                        self.mm(o, til["Ke"][:, h * 64:(h + 1) * 64], vbf[:, h * 64:(h + 1) * 64], False, False)
                        self.mm(o, Tcw[0:64, hi_, :], STb[0:64, h, :], False, True)
                    for hi_ in range(4):
                        h = hg * 4 + hi_
                        self.stt("dve", ST[0:64, h, :], ST[0:64, h, :], PCc[0:64, h:h + 1], bG[0:64, hi_ * 64:(hi_ + 1) * 64], ALU.mult, ALU.add)
                    self.copy("pool", STb[0:64, hg * 4:(hg + 1) * 4, :], ST[0:64, hg * 4:(hg + 1) * 4, :])
                if not own or STOP <= 5:
                    continue
                if to == 0:
                    self.dbg("y", y32, [128, 1024])
                self.reduce(st16["s1"], v3(y32), ALU.add)
                self.tt("pool", W["junk"], y32, y32, ALU.mult)
                self.reduce(st16["s2"], v3(W["junk"]), ALU.add)
                self.ts("dve", st16["mean"], st16["s1"], 1.0 / 64, None, ALU.mult)
                self.tt("dve", st16["m2"], st16["mean"], st16["mean"], ALU.mult)
                self.stt("dve", st16["var"], st16["s2"], 1.0 / 64, st16["m2"], ALU.mult, ALU.subtract)
                self.act(st16["rstd"], st16["var"], AF.Sqrt, bias=self.epsc[GN_EPS][:, 0:1], scale=1.0)
                self.recip(st16["rstd"], st16["rstd"])
                self.tt("dve", v3(y32), v3(y32), bc16(st16["mean"]), ALU.subtract)
                self.tt("dve", v3(y32), v3(y32), bc16(st16["rstd"]), ALU.mult)
                self.tt("pool", y32, y32, cb["lng"], ALU.mult)
                self.tt("pool", y32, y32, cb["lnb"], ALU.add)
                self.tt("dve", W["junk"], r32, k32, ALU.mult)
                self.tt("dve", W["junk"], W["junk"], cb["rk"], ALU.mult)
                self.reduce(st16["bon"], v3(W["junk"]), ALU.add)
                self.tt("dve", v3(W["junk"]), v3(v32), bc16(st16["bon"]), ALU.mult)
                self.tt("dve", y32, y32, W["junk"], ALU.add)
                self.tt("dve", ygb, y32, gbf, ALU.mult)
                bk = self.bank()
                bkb = bk.v(bk.ap.bitcast(BF16))
                for c in range(8):
                    self.tr(bkb[:, c * 128:(c + 1) * 128], ygb[:, c * 128:(c + 1) * 128], self.ident)
                self.copy("act", ygT, bkb.v(bkb.ap[:, 0:1024].rearrange("p (c s) -> p c s", c=8)))
                pd2 = proj_tm(ygT, wo)
                self.postnorm_res(W, pd2, xt, W["GB"], self.xtile(dst[1], dst[0], to))
            if seq:
                self.dma("sp", T(st_out, Buf("x_stout")), ST)
            self.S.flush()
            self.setup = True

def kc_layout(w):
    K, N = w.shape
    return np.ascontiguousarray(w.reshape(K // 128, 128, N).transpose(1, 0, 2))


def col_layout(v):
    return np.ascontiguousarray(v.reshape(-1, 128).T)


def host_weights(inp):
    Wd = {}
    Wd["c_ident"] = np.eye(128, dtype=np.float32)
    Wd["c_masku"] = np.triu(np.ones((128, 128), np.float32))
    for i in range(DEPTH):
        g = inp["norm_gains"][i]
        Wd["gcol%d" % i] = np.ascontiguousarray(g.reshape(6, 8, 128).transpose(2, 0, 1))
        Wd["grow%d" % i] = np.ascontiguousarray(g)
        Wd["memg%d" % i] = col_layout(inp["mem_norm_gains"][i])
        Wd["xa_wq%d" % i] = kc_layout(inp["xa_wq"][i])
        Wd["xa_wkv%d" % i] = kc_layout(inp["xa_wkv"][i])
        Wd["xa_wo%d" % i] = kc_layout(inp["xa_wo"][i])
        Wd["ffn_win%d" % i] = kc_layout(inp["ffn_w_in"][i])
        Wd["ffn_wout%d" % i] = kc_layout(inp["ffn_w_out"][i])
    for j in range(2):
        Wd["gm_win%d" % j] = kc_layout(inp["gm_w_in"][j])
        Wd["gm_wout%d" % j] = kc_layout(inp["gm_w_out"][j])
        Wd["gm_wsT%d" % j] = np.ascontiguousarray(inp["gm_w_s"][j].transpose(2, 0, 1))
        Wd["gm_bin%d" % j] = np.ascontiguousarray(inp["gm_b_in"][j][None, :])
        Wd["gm_bucol%d" % j] = col_layout(inp["gm_b_in"][j][:1024])
        Wd["gm_lng%d" % j] = np.ascontiguousarray(inp["gm_ln_g"][j][None, :])
        Wd["gm_lnb%d" % j] = np.ascontiguousarray(inp["gm_ln_b"][j][None, :])
        Wd["gm_bs%d" % j] = np.ascontiguousarray(inp["gm_b_s"][j][None, :, :])

    m = np.triu(np.ones((128, 128), np.float32))
    Wd["c_tri_incl"] = m
    Wd["c_tri_excl"] = np.triu(np.ones((128, 128), np.float32), 1)
    Wd["c_tri_rem"] = np.tril(np.ones((128, 128), np.float32), -1)
    rep4 = lambda a: np.ascontiguousarray(a)
    ii = np.arange(128)
    for nm, kbig in (("o16", 16), ("o32", 32), ("o64", 64), ("o128", 128)):
        Wd["c_hm_" + nm] = ((ii[:, None] // kbig == ii[None, :] // kbig) & (ii[:, None] // (kbig // 2) != ii[None, :] // (kbig // 2))).astype(np.float32)
    Wd["c_hm_d8"] = (ii[:, None] // 8 == ii[None, :] // 8).astype(np.float32)
    Wd["c_mask_su4"] = rep4(np.triu(np.ones((128, 128), np.float32), 1))
    Wd["c_mask_iu4"] = rep4(np.triu(np.ones((128, 128), np.float32)))
    Wd["c_mask_sl4"] = rep4(np.tril(np.ones((128, 128), np.float32), -1))
    for j in range(2):
        Wd["rw_mixcol%d" % j] = np.ascontiguousarray(inp["rw_mix"][j].reshape(6, 8, 128).transpose(2, 0, 1))
        for n, p in (("wr", 0), ("wk", 1), ("wv", 2)):
            Wd["rw_%s%d" % (n, j)] = kc_layout(inp["rw_w_rkv"][j][p])
        Wd["rw_wo%d" % j] = kc_layout(inp["rw_w_o"][j])
        Wd["rw_w1%d" % j] = kc_layout(inp["rw_w1"][j]); Wd["rw_a1%d" % j] = kc_layout(inp["rw_a1"][j]); Wd["rw_g1%d" % j] = kc_layout(inp["rw_g1"][j])
        Wd["rw_w2%d" % j] = np.ascontiguousarray(inp["rw_w2"][j]); Wd["rw_a2%d" % j] = np.ascontiguousarray(inp["rw_a2"][j])
        Wd["rw_g2%d" % j] = np.ascontiguousarray(inp["rw_g2"][j])
        Wd["rw_w0%d" % j] = np.ascontiguousarray(inp["rw_w0"][j][None, :]); Wd["rw_a0%d" % j] = np.ascontiguousarray(inp["rw_a0"][j][None, :])
        Wd["rw_kk%d" % j] = np.ascontiguousarray(inp["rw_k_k"][j][None, :]); Wd["rw_ka%d" % j] = np.ascontiguousarray(inp["rw_k_a"][j][None, :])
        Wd["rw_rk%d" % j] = np.ascontiguousarray(inp["rw_r_k"][j].reshape(1, 1024))
        Wd["rw_lng%d" % j] = np.ascontiguousarray(inp["rw_ln_g"][j][None, :]); Wd["rw_lnb%d" % j] = np.ascontiguousarray(inp["rw_ln_b"][j][None, :])
    Wd["rw_v1"] = kc_layout(inp["rw_v1"][0]); Wd["rw_v2"] = np.ascontiguousarray(inp["rw_v2"][0]); Wd["rw_v0"] = np.ascontiguousarray(inp["rw_v0"][0][None, :])
    return Wd


def build_program(plan, TOK):
    B = Builder(TOK)
    nc = B.nc
    x_in = B.inp("x_in", [TOK, D]) if plan[0][0] != "rwkv" else None
    x_out = B.outp("x_out", [TOK, D])
    with contextlib.ExitStack() as st:
        blk = st.enter_context(nc.Block())

        def body(_):
            with contextlib.ExitStack() as st2:
                B.start(st2)
                src = ("in", x_in)
                xscr = B.outp("xscr", [TOK, D]) if len(plan) > 1 else None
                for pi, (kind, layer) in enumerate(plan):
                    dst = ("out", x_out) if pi == len(plan) - 1 else ("scr", xscr)
                    if kind == "gmlp":
                        B.phase_gmlp(layer, src, dst)
                    elif kind == "xattn":
                        B.phase_xattn(layer, src, dst)
                    elif kind == "ffn":
                        B.phase_ffn(layer, src, dst)
                    elif kind == "rwkv":
                        B.phase_rwkv(layer, dst)
                    else:
                        raise ValueError(kind)
                    src = dst
                B.S.flush()
                B.S.final_wait("sp")
                print("sched: ops", len(B.S.ops), "waits", B.S.nwait, "cnt", B.S.ecnt, "dma", B.S.ndma, "dsems", len(B.S.dsem))

        blk.sync(body)
    return B


def run_plan(plan, TOK, xs, mems, Wd, extra=None):
    B = build_program(plan, TOK)
    in_maps = []
    for c in range(NCORES):
        m = {}
        for name, (ap, shape) in B.din.items():
            if name == "x_in":
                m[name] = xs[c]
            elif name == "mem":
                m[name] = mems[c]
            elif extra is not None and name in extra:
                m[name] = extra[name][c]
            else:
                a = Wd[name]
                assert tuple(a.shape) == tuple(shape), (name, a.shape, shape)
                m[name] = a
        in_maps.append(m)
    res = run_bass_kernel_spmd(B.nc, in_maps, core_ids=list(range(NCORES)))
    return res.results


def _rwkv_layer(layer, xseg, vf, per, TOK, mems, Wd):
    xprev = []
    for c in range(NCORES):
        xprev.append(np.zeros((128, D), np.float32) if c % per == 0 else np.ascontiguousarray(xseg[c - 1][-128:, :]))
    zero_state = np.zeros((64, 16, 64), np.float32)
    last_out = [None] * NCORES
    x_new = [None] * NCORES
    vf_new = [None] * NCORES
    for q in range(per):
        st_in = []
        for c in range(NCORES):
            if c % per == q and q > 0:
                st_in.append(np.ascontiguousarray(last_out[c - 1]))
            else:
                st_in.append(zero_state)
        extra = {"xin": xseg, "xprev": xprev, "st_in": st_in}
        if vf is not None:
            extra["vf_in"] = vf
        res = run_plan([("rwkv", layer)], TOK, xseg, mems, Wd, extra)
        for c in range(NCORES):
            last_out[c] = res[c]["st_out"]
            if c % per == q:
                x_new[c] = res[c]["x_out"]
                if vf is None:
                    vf_new[c] = res[c]["vf_out"]
    return x_new, vf_new


def _pad4(segs, c, per, TOK):
    b, q = c // per, c % per
    out = np.zeros((4 * TOK, D), np.float32)
    for qq in range(q + 1):
        out[(3 - q + qq) * TOK:(4 - q + qq) * TOK, :] = segs[b * per + qq]
    return out


def kernel(**inputs):
    inp = {k: np.asarray(v) for k, v in inputs.items()}
    x = inp["x"]
    Bsz, S, _ = x.shape
    per = NCORES // Bsz
    TOK = S // per
    xs = [np.ascontiguousarray(x[c // per, (c % per) * TOK:(c % per + 1) * TOK, :]) for c in range(NCORES)]
    mems = [np.ascontiguousarray(inp["mem"][c // per]) for c in range(NCORES)]
    Wd = host_weights(inp)
    res = run_plan([("gmlp", 0), ("xattn", 0), ("ffn", 0)], TOK, xs, mems, Wd)
    x1 = [res[c]["x_out"] for c in range(NCORES)]
    x1r, vf = _rwkv_layer(1, x1, None, per, TOK, mems, Wd)
    res = run_plan([("xattn", 1), ("ffn", 1), ("gmlp", 2), ("xattn", 2), ("ffn", 2)], TOK, x1r, mems, Wd)
    x3 = [res[c]["x_out"] for c in range(NCORES)]
    x3r, _ = _rwkv_layer(3, x3, vf, per, TOK, mems, Wd)
    res = run_plan([("xattn", 3), ("ffn", 3)], TOK, x3r, mems, Wd)
    out = np.empty_like(x)
    for c in range(NCORES):
        out[c // per, (c % per) * TOK:(c % per + 1) * TOK, :] = res[c]["x_out"]
    return out
```

```python
import contextlib
import numpy as np
import concourse.bass as bass
import concourse.mybir as mybir
from concourse.bass_utils import run_bass_kernel_spmd

F32 = mybir.dt.float32
BF16 = mybir.dt.bfloat16
AF = mybir.ActivationFunctionType
ALU = mybir.AluOpType
AX = mybir.AxisListType

D = 1024
DFF = 2816
NMEM = 256
RMS_EPS = 1e-6
LN_EPS = 1e-5
GN_EPS = 64e-5
DEPTH = 4
NCORES = 8
CDEC = -0.6065306597126334


class Buf:
    __slots__ = ("name", "w", "r")

    def __init__(self, name):
        self.name = name
        self.w = None
        self.r = []


class T:
    __slots__ = ("ap", "bufs")

    def __init__(self, ap, bufs):
        self.ap = ap
        self.bufs = bufs if isinstance(bufs, (list, tuple)) else [bufs]

    def __getitem__(self, idx):
        return T(self.ap[idx], self.bufs)

    def v(self, ap):
        return T(ap, self.bufs)


class Sched:
    ENG = ("pe", "act", "dve", "pool", "sp")

    def __init__(self, nc, stack):
        self.nc = nc
        self.stack = stack
        self.h = {"pe": nc.tensor, "act": nc.scalar, "dve": nc.vector, "pool": nc.gpsimd, "sp": nc.sync}
        self.ops = []
        self.done = 0
        self.esem = {e: stack.enter_context(nc.semaphore("s_" + e)) for e in self.ENG}
        self.ecnt = {e: 0 for e in self.ENG}
        self.dsem = {}
        self.dcnt = {}
        self.tok = []
        self.seen = {e: {} for e in self.ENG}
        self.pending_bar = None
        self.egen = {}
        import os
        self.SEM_LIMIT = int(os.environ.get("SEM_LIMIT", "30000"))
        self.nwait = 0
        self.ndma = 0

    def op(self, eng, fn, reads=(), writes=(), dma=None):
        rb = [b for x in reads for b in x.bufs]
        wb = [b for x in writes for b in x.bufs]
        if dma is True:
            dma = (wb[0].name if wb else "st_" + rb[0].name)
        self.ops.append((eng, fn, rb, wb, dma))
        return len(self.ops) - 1

    def _dsem(self, key):
        if key not in self.dsem:
            self.dsem[key] = self.stack.enter_context(self.nc.semaphore("d%d" % len(self.dsem)))
            self.dcnt[key] = 0
        return self.dsem[key]

    def flush(self, barrier=True):
        ops = self.ops
        lo, n = self.done, len(self.ops)
        deps = {}
        needed = set()
        for i in range(lo, n):
            eng, fn, rd, wr, dma = ops[i]
            d = set()
            for b in rd:
                if b.w is not None:
                    d.add(b.w)
            for b in wr:
                if b.w is not None:
                    d.add(b.w)
                d.update(b.r)
            for b in rd:
                b.r.append(i)
            for b in wr:
                b.w = i
                b.r = []
            dd = []
            for j in d:
                if j == i:
                    continue
                ej, _, _, _, dj = ops[j]
                if dj is None and dma is None and ej == eng and eng == "pe":
                    continue
                if dj is not None and dma is not None and dj == dma:
                    continue
                dd.append(j)
                if j >= lo:
                    needed.add(j)
            deps[i] = dd
        last = {}
        for i in range(lo, n):
            if ops[i][4] is None:
                last[ops[i][0]] = i
        if barrier:
            needed.update(last.values())
        first = {e: True for e in self.ENG}
        gfinal = {}
        for i in range(lo, n):
            k = ops[i][4]
            if isinstance(k, str) and k.startswith("G:"):
                self._dsem(k)
                gfinal[k] = gfinal.get(k, self.dcnt[k]) + 16
        for i in range(lo, n):
            eng, fn, rd, wr, dma = ops[i]
            h = self.h[eng]
            w = {}
            dl = list(deps[i])
            if first[eng] and self.pending_bar is not None:
                dl += self.pending_bar
            first[eng] = False
            for j in dl:
                t = self.tok[j]
                if t is None:
                    continue
                s, v, k = t
                if k not in w or w[k][1] < v:
                    w[k] = (s, v)
            for k, (s, v) in w.items():
                if self.seen[eng].get(k, 0) >= v:
                    continue
                h.wait_ge(s, v)
                self.nwait += 1
                self.seen[eng][k] = v
            ins = fn()
            if dma is not None:
                s = self._dsem(dma)
                self.dcnt[dma] += 16
                ins.then_inc(s, 16)
                self.tok.append((s, gfinal.get(dma, self.dcnt[dma]), "d" + str(dma)))
                self.ndma += 1
            elif i in needed:
                if self.ecnt[eng] >= self.SEM_LIMIT:
                    self.egen[eng] = self.egen.get(eng, 0) + 1
                    self.esem[eng] = self.stack.enter_context(self.nc.semaphore("s_%s_%d" % (eng, self.egen[eng])))
                    self.ecnt[eng] = 0
                self.ecnt[eng] += 1
                ins.then_inc(self.esem[eng], 1)
                self.tok.append((self.esem[eng], self.ecnt[eng], "e%s_%d" % (eng, self.egen.get(eng, 0))))
            else:
                self.tok.append(None)
        if barrier:
            bar = list(last.values())
            lastd = {}
            for i in range(lo, n):
                if ops[i][4] is not None:
                    lastd[ops[i][4]] = i
            bar += list(lastd.values())
            if self.pending_bar is not None:
                for e in self.ENG:
                    if first[e]:
                        bar += [j for j in self.pending_bar]
                        break
            self.pending_bar = bar
        self.done = n

    def final_wait(self, eng):
        h = self.h[eng]
        if self.pending_bar:
            w = {}
            for j in self.pending_bar:
                t = self.tok[j]
                if t is None:
                    continue
                s, v, k = t
                if k not in w or w[k][1] < v:
                    w[k] = (s, v)
            for k, (s, v) in w.items():
                h.wait_ge(s, v)


class Builder:
    def __init__(self, TOK):
        self.TOK = TOK
        self.NT = TOK // 128
        self.nc = bass.Bass("TRN2", target_bir_lowering=False)
        self.din = {}
        self.dout = {}
        self._rr1 = 0
        self._rr2 = 0
        self.setup = True

    def inp(self, name, shape, dt=F32):
        if name not in self.din:
            self.din[name] = (self.nc.dram_tensor(name, list(shape), dt, kind="ExternalInput").ap(), tuple(shape))
        return self.din[name][0]

    def outp(self, name, shape, dt=F32):
        self.dout[name] = self.nc.dram_tensor(name, list(shape), dt, kind="ExternalOutput").ap()
        return self.dout[name]

    def sb(self, st, name, shape, dt):
        self._uid = getattr(self, "_uid", 0) + 1
        name = "%s_%d" % (name, self._uid)
        t = st.enter_context(self.nc.sbuf_tensor(name, list(shape), dt))
        return T(t[tuple(slice(None) for _ in shape)], Buf(name))

    def bank(self):
        i = self._rr1 % 8
        self._rr1 += 1
        return self.banks[i]

    def dbl(self):
        i = self._rr2 % 4
        self._rr2 += 1
        return self.dbls[i]

    def mm(self, out, lhsT, rhs, start, stop):
        nc = self.nc
        self.S.op("pe", lambda: nc.tensor.matmul(out.ap, lhsT=lhsT.ap, rhs=rhs.ap, start=start, stop=stop),
                  reads=[lhsT, rhs], writes=[out])

    def tr(self, out, in_, ident):
        nc = self.nc
        self.S.op("pe", lambda: nc.tensor.transpose(out.ap, in_.ap, ident.ap), reads=[in_, ident], writes=[out])

    def act(self, out, in_, func, bias=None, scale=None, accum=None):
        nc = self.nc
        kw = {}
        rd = [in_]
        wr = [out]
        if bias is not None:
            if isinstance(bias, T):
                kw["bias"] = bias.ap
                rd.append(bias)
            else:
                kw["bias"] = bias
        if scale is not None:
            if isinstance(scale, T):
                kw["scale"] = scale.ap
                rd.append(scale)
            else:
                kw["scale"] = scale
        if accum is not None:
            kw["accum_out"] = accum.ap
            wr.append(accum)
        self.S.op("act", lambda: nc.scalar.activation(out=out.ap, in_=in_.ap, func=func, **kw), reads=rd, writes=wr)

    def _e(self, eng):
        return self.nc.vector if eng == "dve" else self.nc.gpsimd

    def tt(self, eng, out, in0, in1, op):
        e = self._e(eng)
        self.S.op(eng, lambda: e.tensor_tensor(out=out.ap, in0=in0.ap, in1=in1.ap, op=op), reads=[in0, in1], writes=[out])

    def ts(self, eng, out, in0, s1, s2, op0, op1=None, accum=None):
        e = self._e(eng)
        rd = [in0]
        wr = [out]
        a1 = s1.ap if isinstance(s1, T) else s1
        a2 = s2.ap if isinstance(s2, T) else s2
        if isinstance(s1, T):
            rd.append(s1)
        if isinstance(s2, T):
            rd.append(s2)
        kw = {}
        if op1 is not None:
            kw["op1"] = op1
        if accum is not None:
            kw["accum_out"] = accum.ap
            wr.append(accum)
        self.S.op(eng, lambda: e.tensor_scalar(out=out.ap, in0=in0.ap, scalar1=a1, scalar2=a2, op0=op0, **kw), reads=rd, writes=wr)

    def stt(self, eng, out, in0, scalar, in1, op0, op1):
        e = self._e(eng)
        rd = [in0, in1]
        sc = scalar.ap if isinstance(scalar, T) else scalar
        if isinstance(scalar, T):
            rd.append(scalar)
        self.S.op(eng, lambda: e.scalar_tensor_tensor(out=out.ap, in0=in0.ap, scalar=sc, in1=in1.ap, op0=op0, op1=op1),
                  reads=rd, writes=[out])

    def copy(self, eng, out, in_):
        if eng == "act":
            self.act(out, in_, AF.Copy)
        else:
            e = self._e(eng)
            self.S.op(eng, lambda: e.tensor_copy(out=out.ap, in_=in_.ap), reads=[in_], writes=[out])

    def memset(self, eng, out, val):
        e = self._e(eng)
        self.S.op(eng, lambda: e.memset(out.ap, val), writes=[out])

    def reduce(self, out, in_, op):
        nc = self.nc
        self.S.op("dve", lambda: nc.vector.tensor_reduce(out=out.ap, in_=in_.ap, axis=AX.X, op=op), reads=[in_], writes=[out])

    def recip(self, out, in_):
        nc = self.nc
        self.S.op("dve", lambda: nc.vector.reciprocal(out=out.ap, in_=in_.ap), reads=[in_], writes=[out])

    def dma(self, q, out, in_):
        e = self.nc.sync if q == "sp" else self.nc.gpsimd
        key = "G:wl" if self.setup else True
        on = out.bufs[0].name
        if on.startswith("x_") or on.startswith("vfo_"):
            key = "st_" + in_.bufs[0].name
        self.S.op(q, lambda: e.dma_start(out=out.ap, in_=in_.ap), reads=[in_], writes=[out], dma=key)

    def dbg(self, name, t, shape):
        import os
        if not os.environ.get("RW_DBG"):
            return
        o = self.outp("dbg_" + name, shape)
        self.dma("pool", T(o, Buf("x_dbg_" + name)), t)

    def start(self, st):
        nc = self.nc
        self.S = Sched(nc, st)
        self.dbls = []
        self.banks = []
        for j in range(4):
            t = st.enter_context(nc.psum_tensor("psd%d" % j, [128, 1024], F32))
            b0, b1 = Buf("ps%d" % (2 * j)), Buf("ps%d" % (2 * j + 1))
            self.dbls.append(T(t[:, :], [b0, b1]))
            self.banks.append(T(t[:, 0:512], [b0]))
            self.banks.append(T(t[:, 512:1024], [b1]))
        self.ident = self.sb(st, "ident", [128, 128], BF16)
        self.ones_bf = self.sb(st, "ones_bf", [128, 128], BF16)
        self.dma("pool", self.ident, T(self.inp("c_ident", [128, 128]), Buf("c_ident")))
        self.memset("dve", self.ones_bf, 1.0)
        self.epsc = {}
        for i, e in enumerate((RMS_EPS, LN_EPS, GN_EPS)):
            self.epsc[e] = self.sb(st, "eps%d" % i, [128, 1], F32)
            self.memset("dve", self.epsc[e], e)
        self.xbufs = {}

    def xtile(self, ap2d, name, t):
        key = (name, t)
        if key not in self.xbufs:
            self.xbufs[key] = Buf("x_%s_%d" % (name, t))
        return T(ap2d[t * 128:(t + 1) * 128, :], self.xbufs[key])

    def wload(self, dst, name, shape, q="pool"):
        src = self.inp(name, shape)
        if len(shape) == 3 and shape[1] * shape[2] > 8192:
            for kc in range(shape[1]):
                self.dma(q, dst[:, kc, :], T(src[:, kc, :], Buf(name)))
        else:
            self.dma(q, dst, T(src, Buf(name)))

    def stats_rstd(self, ss, rs, eps):
        self.act(rs, ss, AF.Sqrt, bias=self.epsc[eps][:, 0:1], scale=1.0 / D)
        self.recip(rs, rs)

    def prenorm_T(self, W, xt, gcol, hT, off):
        self.act(W["junk"], xt, AF.Square, accum=W["ss"])
        self.stats_rstd(W["ss"], W["rs"], RMS_EPS)
        self.act(W["hb"], xt, AF.Copy, scale=W["rs"][:, 0:1])
        bk = self.bank()
        bkb = bk.v(bk.ap.bitcast(BF16))
        for kc in range(8):
            self.tr(bkb[:, kc * 128:(kc + 1) * 128], W["hb"][:, kc * 128:(kc + 1) * 128], self.ident)
        src = bkb.v(bkb.ap[:, 0:1024].rearrange("p (k t) -> p k t", k=8))
        self.tt("dve", hT[:, :, off:off + 128], src, gcol.v(gcol.ap.broadcast_to([128, 8, 128])), ALU.mult)

    def postnorm_res(self, W, pd, xt, GB, dst):
        self.act(W["junk"], pd, AF.Square, accum=W["ss2"])
        self.stats_rstd(W["ss2"], W["rs2"], RMS_EPS)
        self.stt("dve", W["tmp"], pd, W["rs2"][:, 0:1], GB, ALU.mult, ALU.mult)
        self.tt("pool", xt, xt, W["tmp"], ALU.add)
        self.dma("sp", dst, xt)

    def common_work(self, st, nx=4):
        W = {}
        W["junk"] = self.sb(st, "junk", [128, 1024], F32)
        W["tmp"] = self.sb(st, "tmp", [128, 1024], F32)
        W["hb"] = self.sb(st, "hb", [128, 1024], BF16)
        for n in ("ss", "rs", "ss2", "rs2"):
            W[n] = self.sb(st, n, [128, 1], F32)
        W["xt"] = [self.sb(st, "xt%d" % i, [128, 1024], F32) for i in range(nx)]
        W["hT"] = self.sb(st, "hT", [128, 8, 512], BF16)
        W["GB"] = self.sb(st, "GB", [128, 1024], F32)
        W["gcol"] = self.sb(st, "gcol", [128, 8], F32)
        return W

    def load_gains(self, W, layer, pre_idx, post_idx):
        g = self.inp("gcol%d" % layer, [128, 6, 8])
        self.dma("sp", W["gcol"], T(g[:, pre_idx, :], Buf("gcol_in")))
        gr = self.inp("grow%d" % layer, [6, 1024])
        self.dma("sp", W["GB"], T(gr[post_idx:post_idx + 1, :].broadcast_to([128, 1024]), Buf("grow_in")))

    def groups(self):
        G = min(4, self.NT)
        return [(g0, min(G, self.NT - g0)) for g0 in range(0, self.NT, G)]

    def phase_gmlp(self, layer, src, dst):
        j = layer // 2
        with contextlib.ExitStack() as st:
            W = self.common_work(st)
            win = self.sb(st, "win", [128, 8, 2048], BF16)
            wout = self.sb(st, "wout", [128, 8, 1024], BF16)
            wsr = self.sb(st, "wsr", [128, 8, 128], BF16)
            wsT = self.sb(st, "wsT", [128, 8, 128], BF16)
            masku = self.sb(st, "masku", [128, 128], BF16)
            bu = self.sb(st, "bu", [128, 8], F32)
            bv = self.sb(st, "bv", [1, 1024], BF16)
            LNG = self.sb(st, "LNG", [128, 1024], F32)
            LB2 = self.sb(st, "LB2", [1, 1024], BF16)
            R2 = self.sb(st, "R2", [1, 8, 128], BF16)
            BS2 = self.sb(st, "BS2", [1, 8, 128], BF16)
            uT = self.sb(st, "uT", [128, 8, 512], BF16)
            v32 = self.sb(st, "v32", [128, 1024], F32)
            t1 = self.sb(st, "t1", [128, 1024], F32)
            vh = self.sb(st, "vh", [128, 1024], BF16)
            gT = self.sb(st, "gT", [128, 8, 128], BF16)
            sm = {n: self.sb(st, "g_" + n, [128, 1], F32) for n in ("sv", "sv2", "mean", "m2", "var", "rstd")}
            import os
            self.load_gains(W, layer, 0, 1)
            if int(os.environ.get("GM_STOP", "9")) == -3:
                self.S.flush(); self.setup = True
                return
            self.wload(win, "gm_win%d" % j, [128, 8, 2048])
            self.wload(wout, "gm_wout%d" % j, [128, 8, 1024])
            self.wload(wsr, "gm_wsT%d" % j, [128, 8, 128])
            self.dma("pool", masku, T(self.inp("c_masku", [128, 128]), Buf("c_masku")))
            bin_ = self.inp("gm_bin%d" % j, [1, 2048])
            self.dma("sp", bu, T(self.inp("gm_bucol%d" % j, [128, 8]), Buf("bucol")))
            self.dma("pool", bv, T(bin_[0:1, 1024:2048], Buf("bin")))
            self.dma("sp", LNG, T(self.inp("gm_lng%d" % j, [1, 1024]).broadcast_to([128, 1024]), Buf("lng")))
            self.dma("pool", LB2[0:1, :], T(self.inp("gm_lnb%d" % j, [1, 1024]), Buf("lnb")))

            if int(os.environ.get("GM_STOP", "9")) == -2:
                self.S.flush(); self.setup = True
                return
            for g in range(8):
                self.tt("dve", wsT[:, g, :], wsr[:, g, :], masku, ALU.mult)
            bkA = self.bank()
            for g in range(4):
                self.mm(bkA[0:1, g * 128:(g + 1) * 128], self.ones_bf[:, 0:1], wsT[:, g, :], True, True)
            self.copy("dve", R2[0:1, 0:4, :], bkA.v(bkA.ap[0:1, 0:512].rearrange("p (g t) -> p g t", g=4)))
            bkB = self.bank()
            for g in range(4):
                self.mm(bkB[0:1, g * 128:(g + 1) * 128], self.ones_bf[:, 0:1], wsT[:, 4 + g, :], True, True)
            self.copy("dve", R2[0:1, 4:8, :], bkB.v(bkB.ap[0:1, 0:512].rearrange("p (g t) -> p g t", g=4)))

            if int(os.environ.get("GM_STOP", "9")) == -1:
                self.S.flush(); self.setup = True
                return
            bsr = self.inp("gm_bs%d" % j, [1, 8, 128])
            self.dma("pool", BS2, T(bsr, Buf("bs")))
            self.setup = False
            import os
            GS = int(os.environ.get("GM_STOP", "9"))
            if GS <= 0:
                self.S.flush()
                self.setup = True
                return

            for (g0, gn) in self.groups():
                ntok = gn * 128
                for t in range(gn):
                    self.dma("sp", W["xt"][t], self.xtile(src[1], src[0], g0 + t))
                    self.prenorm_T(W, W["xt"][t], W["gcol"], W["hT"], t * 128)
                for n in range(8):
                    bk = self.bank()
                    for kc in range(8):
                        self.mm(bk[:, 0:ntok], win[:, kc, n * 128:(n + 1) * 128], W["hT"][:, kc, 0:ntok], kc == 0, kc == 7)
                    self.act(uT[:, n, 0:ntok], bk[:, 0:ntok], AF.Gelu, bias=bu[:, n:n + 1])
                for t in range(gn):
                    pd = self.dbl()
                    for half in range(2):
                        o = pd[:, half * 512:(half + 1) * 512]
                        for kc in range(8):
                            self.mm(o, W["hT"][:, kc, t * 128:(t + 1) * 128], win[:, kc, 1024 + half * 512:1024 + (half + 1) * 512], kc == 0, False)
                        self.mm(o, self.ones_bf[0:1, 0:128], bv[0:1, half * 512:(half + 1) * 512], False, True)
                    self.act(v32, pd, AF.Gelu, accum=sm["sv"])
                    self.act(W["junk"], v32, AF.Square, accum=sm["sv2"])
                    self.ts("dve", sm["mean"], sm["sv"], 1.0 / D, None, ALU.mult)
                    self.tt("dve", sm["m2"], sm["mean"], sm["mean"], ALU.mult)
                    self.stt("dve", sm["var"], sm["sv2"], 1.0 / D, sm["m2"], ALU.mult, ALU.subtract)
                    self.act(sm["rstd"], sm["var"], AF.Sqrt, bias=self.epsc[LN_EPS][:, 0:1], scale=1.0)
                    self.recip(sm["rstd"], sm["rstd"])
                    self.stt("dve", t1, v32, sm["mean"][:, 0:1], LNG, ALU.subtract, ALU.mult)
                    self.act(vh, t1, AF.Copy, scale=sm["rstd"][:, 0:1])
                    for half in range(2):
                        bk = self.bank()
                        for gi in range(4):
                            g = half * 4 + gi
                            o = bk[:, gi * 128:(gi + 1) * 128]
                            self.mm(o, vh[:, g * 128:(g + 1) * 128], wsT[:, g, :], True, False)
                            self.mm(o, LB2[0:1, g * 128:(g + 1) * 128], R2[0:1, g, :], False, False)
                            self.mm(o, self.ones_bf[0:1, 0:128], BS2[0:1, g, :], False, True)
                        self.tt("dve", gT[:, half * 4:(half + 1) * 4, :], bk.v(bk.ap.rearrange("p (g t) -> p g t", g=4)),
                                uT[:, half * 4:(half + 1) * 4, t * 128:(t + 1) * 128], ALU.mult)
                    pd2 = self.dbl()
                    for half in range(2):
                        o = pd2[:, half * 512:(half + 1) * 512]
                        for kc in range(8):
                            self.mm(o, gT[:, kc, :], wout[:, kc, half * 512:(half + 1) * 512], kc == 0, kc == 7)
                    self.postnorm_res(W, pd2, W["xt"][t], W["GB"], self.xtile(dst[1], dst[0], g0 + t))
            self.S.flush()
            self.setup = True

    def phase_xattn(self, layer, src, dst):
        with contextlib.ExitStack() as st:
            W = self.common_work(st)
            wq = self.sb(st, "wq", [128, 8, 1024], BF16)
            wo = self.sb(st, "wo", [128, 8, 1024], BF16)
            kT = self.sb(st, "kT", [128, 8, 256], BF16)
            vtm = self.sb(st, "vtm", [128, 2, 1024], BF16)
            qT = self.sb(st, "qT", [128, 8, 512], BF16)
            pT = self.sb(st, "pT", [128, 8, 512], BF16)
            oT = self.sb(st, "oT", [128, 8, 512], BF16)
            pe32 = self.sb(st, "pe32", [128, 1024], F32)
            pb16 = self.sb(st, "pb16", [128, 1024], BF16)
            mx = self.sb(st, "mx", [128, 4], F32)
            nmx = self.sb(st, "nmx", [128, 4], F32)
            smx = self.sb(st, "smx", [128, 4], F32)
            rsm = self.sb(st, "rsm", [128, 4], F32)
            self.wload(wq, "xa_wq%d" % layer, [128, 8, 1024])
            self.wload(wo, "xa_wo%d" % layer, [128, 8, 1024])
            with contextlib.ExitStack() as st2:
                wkv = self.sb(st2, "wkv", [128, 8, 2048], BF16)
                mT = self.sb(st2, "mT", [128, 8, 256], BF16)
                mg = self.sb(st2, "mg", [128, 8], F32)
                self.wload(wkv, "xa_wkv%d" % layer, [128, 8, 2048])
                self.dma("sp", mg, T(self.inp("memg%d" % layer, [128, 8]), Buf("memg")))
                mem = self.inp("mem", [NMEM, D])
                for mt in range(2):
                    self.dma("sp", W["xt"][mt], T(mem[mt * 128:(mt + 1) * 128, :], Buf("mem_in")))
                    self.prenorm_T(W, W["xt"][mt], mg, mT, mt * 128)
                for n in range(8):
                    bk = self.bank()
                    for kc in range(8):
                        self.mm(bk[:, 0:256], wkv[:, kc, n * 128:(n + 1) * 128], mT[:, kc, :], kc == 0, kc == 7)
                    self.copy("act", kT[:, n, :], bk[:, 0:256])
                for mt in range(2):
                    pd = self.dbl()
                    for half in range(2):
                        o = pd[:, half * 512:(half + 1) * 512]
                        for kc in range(8):
                            self.mm(o, mT[:, kc, mt * 128:(mt + 1) * 128], wkv[:, kc, 1024 + half * 512:1024 + (half + 1) * 512], kc == 0, kc == 7)
                    self.copy("act", vtm[:, mt, :], pd)
                self.S.flush()
            self.load_gains(W, layer, 2, 3)
            self.setup = False
            for (g0, gn) in self.groups():
                ntok = gn * 128
                for t in range(gn):
                    self.dma("sp", W["xt"][t], self.xtile(src[1], src[0], g0 + t))
                    self.prenorm_T(W, W["xt"][t], W["gcol"], W["hT"], t * 128)
                for n in range(8):
                    bk = self.bank()
                    for kc in range(8):
                        self.mm(bk[:, 0:ntok], wq[:, kc, n * 128:(n + 1) * 128], W["hT"][:, kc, 0:ntok], kc == 0, kc == 7)
                    self.copy("act", qT[:, n, 0:ntok], bk[:, 0:ntok])
                import os
                XS = int(os.environ.get("XA_STOP", "9"))
                if XS <= 1:
                    continue
                for t in range(gn):
                    pd = self.dbl()
                    for h in range(4):
                        for half in range(2):
                            self.mm(pd[:, h * 256:(h + 1) * 256], qT[:, 2 * h + half, t * 128:(t + 1) * 128], kT[:, 2 * h + half, :], half == 0, half == 1)
                    self.reduce(mx, pd.v(pd.ap.rearrange("p (h m) -> p h m", h=4)), ALU.max)
                    self.ts("dve", nmx, mx, -1.0 / 16.0, None, ALU.mult)
                    for h in range(4):
                        self.act(pe32[:, h * 256:(h + 1) * 256], pd[:, h * 256:(h + 1) * 256], AF.Exp, bias=nmx[:, h:h + 1],
                                 scale=1.0 / 16.0, accum=smx[:, h:h + 1])
                    if XS <= 2:
                        continue
                    self.recip(rsm, smx)
                    self.tt("dve", pb16.v(pb16.ap.rearrange("p (h m) -> p h m", h=4)), pe32.v(pe32.ap.rearrange("p (h m) -> p h m", h=4)),
                            rsm.v(rsm.ap.broadcast_to([128, 4, 256])), ALU.mult)
                    if XS <= 3:
                        continue
                    bk = self.bank()
                    bkb = bk.v(bk.ap.bitcast(BF16))
                    for c in range(8):
                        self.tr(bkb[:, c * 128:(c + 1) * 128], pb16[:, c * 128:(c + 1) * 128], self.ident)
                    self.copy("act", pT[:, :, t * 128:(t + 1) * 128], bkb.v(bkb.ap[:, 0:1024].rearrange("p (c s) -> p c s", c=8)))
                if XS <= 4:
                    continue
                for n in range(8):
                    h, dh = n // 2, n % 2
                    bk = self.bank()
                    for mh in range(2):
                        self.mm(bk[:, 0:ntok], vtm[:, mh, h * 256 + dh * 128:h * 256 + (dh + 1) * 128], pT[:, 2 * h + mh, 0:ntok], mh == 0, mh == 1)
                    self.copy("act", oT[:, n, 0:ntok], bk[:, 0:ntok])
                if XS <= 5:
                    continue
                for t in range(gn):
                    pd2 = self.dbl()
                    for half in range(2):
                        o = pd2[:, half * 512:(half + 1) * 512]
                        for kc in range(8):
                            self.mm(o, oT[:, kc, t * 128:(t + 1) * 128], wo[:, kc, half * 512:(half + 1) * 512], kc == 0, kc == 7)
                    self.postnorm_res(W, pd2, W["xt"][t], W["GB"], self.xtile(dst[1], dst[0], g0 + t))
            self.S.flush()
            self.setup = True

    def phase_ffn(self, layer, src, dst):
        NH = DFF // 128
        with contextlib.ExitStack() as st:
            W = self.common_work(st)
            win = self.sb(st, "fwin", [128, 8, 2 * DFF], BF16)
            wout = self.sb(st, "fwout", [128, NH, 1024], BF16)
            hid = self.sb(st, "hid", [128, NH, 512], BF16)
            sg = self.sb(st, "sg", [128, 512], BF16)
            self.load_gains(W, layer, 4, 5)
            self.wload(win, "ffn_win%d" % layer, [128, 8, 2 * DFF])
            self.wload(wout, "ffn_wout%d" % layer, [128, NH, 1024])
            self.setup = False
            for (g0, gn) in self.groups():
                ntok = gn * 128
                for t in range(gn):
                    self.dma("sp", W["xt"][t], self.xtile(src[1], src[0], g0 + t))
                    self.prenorm_T(W, W["xt"][t], W["gcol"], W["hT"], t * 128)
                for n in range(NH):
                    bg = self.bank()
                    bu = self.bank()
                    for kc in range(8):
                        self.mm(bg[:, 0:ntok], win[:, kc, n * 128:(n + 1) * 128], W["hT"][:, kc, 0:ntok], kc == 0, kc == 7)
                    for kc in range(8):
                        self.mm(bu[:, 0:ntok], win[:, kc, DFF + n * 128:DFF + (n + 1) * 128], W["hT"][:, kc, 0:ntok], kc == 0, kc == 7)
                    self.act(sg[:, 0:ntok], bg[:, 0:ntok], AF.Silu)
                    self.tt("dve", hid[:, n, 0:ntok], sg[:, 0:ntok], bu[:, 0:ntok], ALU.mult)
                for t in range(gn):
                    pd2 = self.dbl()
                    for half in range(2):
                        o = pd2[:, half * 512:(half + 1) * 512]
                        for kc in range(NH):
                            self.mm(o, hid[:, kc, t * 128:(t + 1) * 128], wout[:, kc, half * 512:(half + 1) * 512], kc == 0, kc == NH - 1)
                    self.postnorm_res(W, pd2, W["xt"][t], W["GB"], self.xtile(dst[1], dst[0], g0 + t))
            self.S.flush()
            self.setup = True


    def phase_rwkv(self, layer, dst, seq=True):
        j = layer // 2
        has_vres = j > 0
        TOK, NT = self.TOK, self.NT
        if seq:
            NT4 = NT
            xin4 = self.inp("xin", [TOK, D])
            xprev = self.inp("xprev", [128, D])
            st_in = self.inp("st_in", [64, 16, 64])
            st_out = self.outp("st_out", [64, 16, 64])
            vf4 = self.inp("vf_in", [TOK, D]) if has_vres else None
        else:
            NT4 = 4 * NT
            xin4 = self.inp("xin4", [4 * TOK, D])
            vf4 = self.inp("vf4", [4 * TOK, D]) if has_vres else None
        vfo = self.outp("vf_out", [TOK, D]) if not has_vres else None
        with contextlib.ExitStack() as st:
            sb = lambda n, sh, dt: self.sb(st, n, sh, dt)
            W = {}
            W["junk"] = sb("junk", [128, 1024], F32)
            W["tmp"] = sb("tmp", [128, 1024], F32)
            for n in ("ss", "rs", "ss2", "rs2"):
                W[n] = sb(n, [128, 1], F32)
            W["GB"] = sb("GB", [128, 1024], F32)
            W["gcol"] = sb("gcol", [128, 8], F32)
            xt = sb("xt", [128, 1024], F32)
            self.load_gains(W, layer, 0, 1)
            import os
            wr = sb("wr", [128, 8, 1024], BF16)
            if os.environ.get("RW_ALIAS"):
                wk = wr; wv = wr; wo = wr
            else:
                wk = sb("wk", [128, 8, 1024], BF16); wv = sb("wv", [128, 8, 1024], BF16); wo = sb("wo", [128, 8, 1024], BF16)
            self.wload(wr, "rw_wr%d" % j, [128, 8, 1024]); self.wload(wk, "rw_wk%d" % j, [128, 8, 1024])
            self.wload(wv, "rw_wv%d" % j, [128, 8, 1024]); self.wload(wo, "rw_wo%d" % j, [128, 8, 1024])
            w1 = sb("w1", [128, 8, 64], BF16); a1 = sb("a1", [128, 8, 64], BF16); g1 = sb("g1", [128, 8, 160], BF16)
            self.wload(w1, "rw_w1%d" % j, [128, 8, 64]); self.wload(a1, "rw_a1%d" % j, [128, 8, 64]); self.wload(g1, "rw_g1%d" % j, [128, 8, 160])
            w2 = sb("w2", [64, 1024], BF16); a2 = sb("a2", [64, 1024], BF16)
            g2a = sb("g2a", [128, 1024], BF16); g2b = sb("g2b", [32, 1024], BF16)
            self.wload(w2, "rw_w2%d" % j, [64, 1024]); self.wload(a2, "rw_a2%d" % j, [64, 1024])
            g2d = self.inp("rw_g2%d" % j, [160, 1024])
            self.dma("pool", g2a, T(g2d[0:128, :], Buf("g2d"))); self.dma("pool", g2b, T(g2d[128:160, :], Buf("g2d2")))
            a0r = sb("a0r", [1, 1024], BF16)
            self.wload(a0r, "rw_a0%d" % j, [1, 1024])
            W0B = sb("W0B", [128, 1024], F32)
            self.dma("sp", W0B, T(self.inp("rw_w0%d" % j, [1, 1024]).broadcast_to([128, 1024]), Buf("w0in")))
            if has_vres:
                v1 = sb("v1", [128, 8, 32], BF16); v2 = sb("v2", [32, 1024], BF16); v0r = sb("v0r", [1, 1024], BF16)
                self.wload(v1, "rw_v1", [128, 8, 32]); self.wload(v2, "rw_v2", [32, 1024]); self.wload(v0r, "rw_v0", [1, 1024])
            mixc = sb("mixc", [128, 6, 8], F32)
            self.dma("sp", mixc, T(self.inp("rw_mixcol%d" % j, [128, 6, 8]), Buf("mixcol")))
            cb = {}
            for n in ("kk", "ka", "rk", "lng", "lnb"):
                lowp = n in ("rk", "lng", "lnb")
                cb[n] = sb("cb_" + n, [128, 1024], BF16 if lowp else F32)
                self.dma("pool" if lowp else "sp", cb[n], T(self.inp("rw_%s%d" % (n, j), [1, 1024]).broadcast_to([128, 1024]), Buf("cbin" + n)))
            msk = {}
            for n in ("su4", "iu4", "sl4"):
                m_ = sb("m_" + n, [128, 128], BF16)
                self.dma("pool", m_, T(self.inp("c_mask_" + n, [128, 128]), Buf("cm" + n)))
                msk[n] = m_.v(m_.ap[:, None, :].broadcast_to([128, 4, 128]))
            tri = {}
            for n in ("incl", "excl", "rem"):
                tri[n] = sb("tri_" + n, [128, 128], BF16)
                self.dma("pool", tri[n], T(self.inp("c_tri_" + n, [128, 128]), Buf("ct" + n)))
            hTc = sb("hTc", [128, 8, 129], BF16)
            XX_ALIAS = True
            r32 = sb("r32", [128, 1024], F32); k32 = sb("k32", [128, 1024], F32); v32 = sb("v32", [128, 1024], F32)
            sg = sb("sg", [128, 1024], F32); a32 = sb("a32", [128, 1024], F32); kkn = sb("kkn", [128, 1024], F32)
            b32 = a32; pex = sg; y32 = W["tmp"]
            hi = sb("hi", [128, 1024], BF16); lo = sb("lo", [128, 1024], BF16); W["hb"] = lo; gbf = sb("gbf", [128, 1024], BF16)
            vbf = sb("vbf", [128, 1024], BF16)
            til = {n: sb("t_" + n, [128, 1024], BF16) for n in ("A", "R", "B", "K", "Be", "Ke")}
            v8 = lambda t_: t_.v(t_.ap.rearrange("p (k t) -> p k t", k=8))
            xx = v8(til["Ke"])
            xm = [v8(til["B"]), v8(til["K"])]
            ygb = hi; ygT = xm[0]
            tT = {n: sb("tT_" + n, [64, 16, 128], BF16) for n in ("A", "R", "B", "K")}
            l1w = sb("l1w", [64, 128], BF16); l1a = sb("l1a", [64, 128], BF16); l1g = sb("l1g", [128, 128], BF16)
            l1g2 = sb("l1g2", [32, 128], BF16); l1v = sb("l1v", [32, 128], BF16)
            st16 = {n: sb("s16_" + n, [128, 16], F32) for n in ("ss", "rn", "s1", "s2", "mean", "m2", "var", "rstd", "bon")}
            LT = [sb("LT0", [128, 4, 128], BF16)]
            Lm = [sb("Lm0", [128, 4, 128], BF16)]
            hb_ = {n: sb("hb_" + n, [128, 4, 128], BF16) for n in ("X", "XT", "X2", "X2T", "T", "TT", "A1", "OkT")}
            hm = {}
            for n in ("d8", "o16", "o32", "o64", "o128"):
                hm[n] = sb("hm_" + n, [128, 128], BF16)
                self.dma("pool", hm[n], T(self.inp("c_hm_" + n, [128, 128]), Buf("chm" + n)))
            MrbT = sb("MrbT", [128, 4, 128], BF16); LakT = sb("LakT", [128, 4, 128], BF16); MrkT = sb("MrkT", [128, 4, 128], BF16)
            Z = sb("Z", [128, 4, 128], BF16)
            Tcw = sb("Tcw", [64, 4, 64], BF16); RmT = sb("RmT", [64, 4, 128], BF16)
            ST = sb("ST", [64, 16, 64], F32); STb = sb("STb", [64, 16, 64], BF16)
            PCc = sb("PCc", [64, 16], F32)
            if seq:
                self.dma("sp", ST, T(st_in, Buf("st_in")))
                self.copy("dve", STb, ST)
            else:
                self.memset("dve", ST, 0.0)
                self.memset("dve", STb, 0.0)
            self.memset("dve", hTc, 0.0)

            def v3(t, n=16):
                return t.v(t.ap.rearrange("p (h j) -> p h j", h=n))

            def bc16(t):
                return t.v(t.ap.broadcast_to([128, 16, 64]))

            def proj_tm(xmT, w, nK=8):
                pd = self.dbl()
                for half in range(2):
                    o = pd[:, half * 512:(half + 1) * 512]
                    for kc in range(nK):
                        self.mm(o, xmT[:, kc, :], w[:, kc, half * 512:(half + 1) * 512], kc == 0, kc == nK - 1)
                return pd

            def mix(p, slot):
                eng = "dve"
                o = xm[slot]
                self.tt(eng, o, xx, mixc.v(mixc.ap[:, p, :].broadcast_to([128, 8, 128])), ALU.mult)
                self.tt(eng, o, o, hTc[:, :, 1:129], ALU.add)
                return o

            def lora1(xmT, wl, n, dstT, func):
                bk = self.bank()
                for kc in range(8):
                    self.mm(bk[0:n, 0:128], wl[:, kc, 0:n] if n <= 128 else None, xmT[:, kc, :], kc == 0, kc == 7)
                self.act(dstT, bk[0:n, 0:128], func)

            self.setup = False
            if seq:
                self.dma("sp", xt, T(xprev, Buf("xprev_in")))
                self.prenorm_T(W, xt, W["gcol"], hTc, 1)
            for ti in range(NT4):
                own = True if seq else ti >= 3 * NT
                to = ti if seq else ti - 3 * NT
                self.dma("sp", xt, T(xin4[ti * 128:(ti + 1) * 128, :], Buf("xin4_%d" % ti)))
                self.copy("dve", hTc[:, :, 0:1], hTc[:, :, 128:129])
                self.prenorm_T(W, xt, W["gcol"], hTc, 1)
                self.tt("dve", xx, hTc[:, :, 0:128], hTc[:, :, 1:129], ALU.subtract)
                STOP = int(os.environ.get("RW_STOP", "9"))
                if STOP <= 1:
                    continue
                if own:
                    m = mix(0, 0)
                    self.copy("act", r32, proj_tm(m, wr))
                m = mix(1, 1)
                self.copy("act", k32, proj_tm(m, wk))
                m = mix(2, 0)
                self.copy("act", v32, proj_tm(m, wv))
                if has_vres:
                    lora1(m, v1, 32, l1v, AF.Copy)
                    pd = self.dbl()
                    for half in range(2):
                        o = pd[:, half * 512:(half + 1) * 512]
                        self.mm(o, l1v[0:32, :], v2[0:32, half * 512:(half + 1) * 512], True, False)
                        self.mm(o, self.ones_bf[0:1, 0:128], v0r[0:1, half * 512:(half + 1) * 512], False, True)
                    self.act(pex, pd, AF.Sigmoid)
                    self.dma("sp", W["tmp"], T(vf4[ti * 128:(ti + 1) * 128, :], Buf("vf4_%d" % ti)))
                    self.tt("dve", W["tmp"], W["tmp"], v32, ALU.subtract)
                    self.tt("dve", W["tmp"], W["tmp"], pex, ALU.mult)
                    self.tt("dve", v32, v32, W["tmp"], ALU.add)
                elif own:
                    self.dma("sp", T(vfo[to * 128:(to + 1) * 128, :], Buf("vfo_%d" % to)), v32)
                self.copy("pool", vbf, v32)
                m = mix(3, 1)
                lora1(m, w1, 64, l1w, AF.Tanh)
                pd = self.dbl()
                for half in range(2):
                    o = pd[:, half * 512:(half + 1) * 512]
                    self.mm(o, l1w[0:64, :], w2[0:64, half * 512:(half + 1) * 512], True, True)
                self.tt("dve", W["junk"], pd, W0B, ALU.add)
                self.act(sg, W["junk"], AF.Sigmoid)
                m = mix(4, 0)
                lora1(m, a1, 64, l1a, AF.Copy)
                pd = self.dbl()
                for half in range(2):
                    o = pd[:, half * 512:(half + 1) * 512]
                    self.mm(o, l1a[0:64, :], a2[0:64, half * 512:(half + 1) * 512], True, False)
                    self.mm(o, self.ones_bf[0:1, 0:128], a0r[0:1, half * 512:(half + 1) * 512], False, True)
                self.act(a32, pd, AF.Sigmoid)
                if own:
                    m = mix(5, 1)
                    lora1(m, g1, 128, l1g, AF.Sigmoid)
                    bk = self.bank()
                    for kc in range(8):
                        self.mm(bk[0:32, 0:128], g1[:, kc, 128:160], m[:, kc, :], kc == 0, kc == 7)
                    self.act(l1g2, bk[0:32, 0:128], AF.Sigmoid)
                    pd = self.dbl()
                    for half in range(2):
                        o = pd[:, half * 512:(half + 1) * 512]
                        self.mm(o, l1g, g2a[:, half * 512:(half + 1) * 512], True, False)
                        self.mm(o, l1g2[0:32, :], g2b[0:32, half * 512:(half + 1) * 512], False, True)
                    self.copy("act", gbf, pd)
                if STOP <= 2:
                    continue
                self.tt("dve", kkn, k32, cb["kk"], ALU.mult)
                self.tt("pool", W["junk"], kkn, kkn, ALU.mult)
                self.reduce(st16["ss"], v3(W["junk"]), ALU.add)
                self.ts("dve", st16["ss"], st16["ss"], 1e-24, None, ALU.max)
                self.act(st16["rn"], st16["ss"], AF.Sqrt)
                self.recip(st16["rn"], st16["rn"])
                self.tt("dve", v3(kkn), v3(kkn), bc16(st16["rn"]), ALU.mult)
                self.stt("dve", W["junk"], a32, -1.0, cb["ka"], ALU.add, ALU.mult)
                self.stt("dve", k32, W["junk"], 1.0, k32, ALU.add, ALU.mult)
                self.tt("dve", b32, kkn, a32, ALU.mult)
                if own and to == 0:
                    self.dbg("r", r32, [128, 1024]); self.dbg("kp", k32, [128, 1024]); self.dbg("v", v32, [128, 1024])
                    self.dbg("sg", sg, [128, 1024]); self.dbg("b", b32, [128, 1024]); self.dbg("kkn", kkn, [128, 1024])
                self.copy("dve", hi, sg)
                self.tt("dve", lo, sg, hi, ALU.subtract)
                def cum(which):
                    pd = self.dbl()
                    for half in range(2):
                        o = pd[:, half * 512:(half + 1) * 512]
                        self.mm(o, tri[which], hi[:, half * 512:(half + 1) * 512], True, False)
                        self.mm(o, tri[which], lo[:, half * 512:(half + 1) * 512], False, True)
                    return pd
                pd = cum("excl")
                self.act(pex, pd, AF.Exp, scale=CDEC)
                self.stt("dve", til["A"], kkn, -1.0, pex, ALU.mult, ALU.mult)
                pd = cum("incl")
                self.act(pex, pd, AF.Exp, scale=CDEC)
                if own and to == 0:
                    self.dbg("Pincl", pex, [128, 1024])
                if own:
                    self.tt("dve", til["R"], r32, pex, ALU.mult)
                self.recip(pex, pex)
                self.tt("dve", til["B"], b32, pex, ALU.mult)
                self.tt("pool", til["K"], k32, pex, ALU.mult)
                pd = cum("rem")
                self.act(pex, pd, AF.Exp, scale=CDEC)
                self.tt("dve", til["Be"], b32, pex, ALU.mult)
                self.tt("pool", til["Ke"], k32, pex, ALU.mult)
                if own and to == 0:
                    for n_ in ("A", "R", "B", "K", "Be", "Ke"):
                        self.dbg("t" + n_, til[n_], [128, 1024])
                bk = self.bank()
                for h in range(16):
                    self.mm(bk[0:64, h:h + 1], hi[:, h * 64:(h + 1) * 64], self.ones_bf[:, 0:1], True, False)
                    self.mm(bk[0:64, h:h + 1], lo[:, h * 64:(h + 1) * 64], self.ones_bf[:, 0:1], False, True)
                self.act(PCc, bk[0:64, 0:16], AF.Exp, scale=CDEC)
                if STOP <= 3:
                    continue
                for n in (("A", "R", "B", "K") if own else ("A", "B", "K")):
                    for hh in range(2):
                        bk = self.bank()
                        bkb = bk.v(bk.ap.bitcast(BF16))
                        for c in range(8):
                            h = hh * 8 + c
                            self.tr(bkb[0:64, c * 128:(c + 1) * 128], til[n][:, h * 64:(h + 1) * 64], self.ident)
                        self.copy("act", tT[n][:, hh * 8:(hh + 1) * 8, :], bkb.v(bkb.ap[0:64, 0:1024].rearrange("p (c s) -> p c s", c=8)))
                if STOP <= 4:
                    continue
                for hg in range(4):
                    bLT, bL, bAK = self.bank(), self.bank(), self.bank()
                    if own:
                        bRB, bRK = self.bank(), self.bank()
                    for hi_ in range(4):
                        h = hg * 4 + hi_
                        pr, base = h // 2, ((h % 2) * 64 if not os.environ.get("RW_BASE0") else 0)
                        At = tT["A"][0:64, h, :]; Rt = tT["R"][0:64, h, :]
                        Bt = tT["B"][0:64, h, :]; Kt = tT["K"][0:64, h, :]
                        sl = slice(hi_ * 128, (hi_ + 1) * 128)
                        self.mm(bLT[:, sl], Bt, At, True, True)
                        self.mm(bL[:, sl], At, Bt, True, True)
                        self.mm(bAK[:, sl], Kt, At, True, True)
                        if own:
                            self.mm(bRB[:, sl], Bt, Rt, True, True)
                            self.mm(bRK[:, sl], Kt, Rt, True, True)
                    g4 = lambda b: b.v(b.ap.rearrange("p (h t) -> p h t", h=4))
                    self.tt("dve", LT[0], g4(bLT), msk["su4"], ALU.mult)
                    self.tt("dve", Lm[0], g4(bL), msk["sl4"], ALU.mult)
                    self.tt("dve", LakT, g4(bAK), msk["su4"], ALU.mult)
                    if own:
                        self.tt("dve", MrbT, g4(bRB), msk["iu4"], ALU.mult)
                        self.tt("dve", MrkT, g4(bRK), msk["iu4"], ALU.mult)
                    if own and to == 0 and hg == 0:
                        self.dbg("LT0", LT[0], [128, 4, 128]); self.dbg("Lm0", Lm[0], [128, 4, 128]); self.dbg("MrbT", MrbT, [128, 4, 128])
                        self.dbg("LakT", LakT, [128, 4, 128]); self.dbg("MrkT", MrkT, [128, 4, 128])
                        self.dbg("tTA", tT["A"], [64, 16, 128]); self.dbg("tTK", tT["K"], [64, 16, 128])
                    bX = self.bank()
                    for hi_ in range(4):
                        h = hg * 4 + hi_
                        self.mm(bX[:, hi_ * 64:(hi_ + 1) * 64], LakT[:, hi_, :], vbf[:, h * 64:(h + 1) * 64], True, True)
                    self.copy("pool", Z[:, :, 0:64], til["A"].v(til["A"].ap[:, hg * 256:(hg + 1) * 256].rearrange("p (h j) -> p h j", h=4)))
                    self.copy("act", Z[:, :, 64:128], bX.v(bX.ap[:, 0:256].rearrange("p (h j) -> p h j", h=4)))
                    if own and to == 0 and hg == 0:
                        self.dbg("Z0", Z, [128, 4, 128])
                    LTf, Lf = LT[0], Lm[0]
                    bc4 = lambda m_: m_.v(m_.ap[:, None, :].broadcast_to([128, 4, 128]))
                    X, XT, X2, X2T, Tm, TT, A1, OkT = hb_["X"], hb_["XT"], hb_["X2"], hb_["X2T"], hb_["T"], hb_["TT"], hb_["A1"], hb_["OkT"]

                    def mm4(bank_, l_, r_):
                        for hi_ in range(4):
                            self.mm(bank_[:, hi_ * 128:(hi_ + 1) * 128], l_[:, hi_, :], r_[:, hi_, :], True, True)

                    self.tt("dve", X, Lf, bc4(hm["d8"]), ALU.mult)
                    self.tt("pool", XT, LTf, bc4(hm["d8"]), ALU.mult)
                    self.tt("dve", Tm, X, bc4(self.ident), ALU.add)
                    self.tt("pool", TT, XT, bc4(self.ident), ALU.add)
                    for rep in range(2):
                        src, srcT = (X, XT) if rep == 0 else (X2, X2T)
                        dstm, dstT = (X2, X2T) if rep == 0 else (X, XT)
                        b2, b2T = self.bank(), self.bank()
                        mm4(b2, srcT, src)
                        mm4(b2T, src, srcT)
                        self.copy("act", dstm, g4(b2))
                        self.copy("act", dstT, g4(b2T))
                        bA, bB = self.bank(), self.bank()
                        mm4(bA, dstT, Tm)
                        mm4(bB, dstm, TT)
                        self.tt("dve", Tm, g4(bA), Tm, ALU.add)
                        self.tt("dve", TT, g4(bB), TT, ALU.add)
                    for kk_ in ("o16", "o32", "o64", "o128"):
                        self.tt("pool", OkT, LTf, bc4(hm[kk_]), ALU.mult)
                        bA = self.bank()
                        mm4(bA, OkT, Tm)
                        self.copy("act", A1, g4(bA))
                        bB, bC = self.bank(), self.bank()
                        mm4(bB, TT, A1)
                        mm4(bC, A1, TT)
                        self.tt("dve", Tm, g4(bB), Tm, ALU.add)
                        self.tt("dve", TT, g4(bC), TT, ALU.add)
                    bZ = self.bank()
                    mm4(bZ, TT, Z)
                    self.copy("act", Z, g4(bZ))
                    if own and to == 0 and hg == 0:
                        self.dbg("Zfin", Z, [128, 4, 128])
                    bT = self.bank()
                    for hi_ in range(4):
                        h = hg * 4 + hi_
                        self.mm(bT[0:64, hi_ * 64:(hi_ + 1) * 64], Z[:, hi_, 0:64], til["Be"][:, h * 64:(h + 1) * 64], True, True)
                    self.copy("act", Tcw, bT.v(bT.ap[0:64, 0:256].rearrange("p (h j) -> p h j", h=4)))
                    if own:
                        bR = self.bank()
                        for hi_ in range(4):
                            h = hg * 4 + hi_
                            sl = slice(hi_ * 128, (hi_ + 1) * 128)
                            self.mm(bR[0:64, sl], Z[:, hi_, 0:64], MrbT[:, hi_, :], True, False)
                            self.mm(bR[0:64, sl], til["R"][:, h * 64:(h + 1) * 64], self.ident, False, True)
                        self.copy("act", RmT, bR.v(bR.ap[0:64, :].rearrange("p (h t) -> p h t", h=4)))
                        bY = self.bank()
                        for hi_ in range(4):
                            h = hg * 4 + hi_
                            o = bY[:, hi_ * 64:(hi_ + 1) * 64]
                            self.mm(o, MrbT[:, hi_, :], Z[:, hi_, 64:128], True, False)
                            self.mm(o, MrkT[:, hi_, :], vbf[:, h * 64:(h + 1) * 64], False, False)
                            self.mm(o, RmT[0:64, hi_, :], STb[0:64, h, :], False, True)
                        self.copy("act", y32[:, hg * 256:(hg + 1) * 256], bY[:, 0:256])
                    bG = self.bank()
                    for hi_ in range(4):
                        h = hg * 4 + hi_
                        o = bG[0:64, hi_ * 64:(hi_ + 1) * 64]
                        self.mm(o, til["Be"][:, h * 64:(h + 1) * 64], Z[:, hi_, 64:128], True, False)
                        self.mm(o, til["Ke"][:, h * 64:(h + 1) * 64], vbf[:, h * 64:(h + 1) * 64], False, False)
                        self.mm(o, Tcw[0:64, hi_, :], STb[0:64, h, :], False, True)
                    for hi_ in range(4):
                        h = hg * 4 + hi_
                        self.stt("dve", ST[0:64, h, :], ST[0:64, h, :], PCc[0:64, h:h + 1], bG[0:64, hi_ * 64:(hi_ + 1) * 64], ALU.mult, ALU.add)
                    self.copy("pool", STb[0:64, hg * 4:(hg + 1) * 4, :], ST[0:64, hg * 4:(hg + 1) * 4, :])
                if not own or STOP <= 5:
                    continue
                if to == 0:
                    self.dbg("y", y32, [128, 1024])
                self.reduce(st16["s1"], v3(y32), ALU.add)
                self.tt("pool", W["junk"], y32, y32, ALU.mult)
                self.reduce(st16["s2"], v3(W["junk"]), ALU.add)
                self.ts("dve", st16["mean"], st16["s1"], 1.0 / 64, None, ALU.mult)
                self.tt("dve", st16["m2"], st16["mean"], st16["mean"], ALU.mult)
                self.stt("dve", st16["var"], st16["s2"], 1.0 / 64, st16["m2"], ALU.mult, ALU.subtract)
                self.act(st16["rstd"], st16["var"], AF.Sqrt, bias=self.epsc[GN_EPS][:, 0:1], scale=1.0)
                self.recip(st16["rstd"], st16["rstd"])
                self.tt("dve", v3(y32), v3(y32), bc16(st16["mean"]), ALU.subtract)
                self.tt("dve", v3(y32), v3(y32), bc16(st16["rstd"]), ALU.mult)
                self.tt("pool", y32, y32, cb["lng"], ALU.mult)
                self.tt("pool", y32, y32, cb["lnb"], ALU.add)
                self.tt("dve", W["junk"], r32, k32, ALU.mult)
                self.tt("dve", W["junk"], W["junk"], cb["rk"], ALU.mult)
                self.reduce(st16["bon"], v3(W["junk"]), ALU.add)
                self.tt("dve", v3(W["junk"]), v3(v32), bc16(st16["bon"]), ALU.mult)
                self.tt("dve", y32, y32, W["junk"], ALU.add)
                self.tt("dve", ygb, y32, gbf, ALU.mult)
                bk = self.bank()
                bkb = bk.v(bk.ap.bitcast(BF16))
                for c in range(8):
                    self.tr(bkb[:, c * 128:(c + 1) * 128], ygb[:, c * 128:(c + 1) * 128], self.ident)
                self.copy("act", ygT, bkb.v(bkb.ap[:, 0:1024].rearrange("p (c s) -> p c s", c=8)))
                pd2 = proj_tm(ygT, wo)
                self.postnorm_res(W, pd2, xt, W["GB"], self.xtile(dst[1], dst[0], to))
            if seq:
                self.dma("sp", T(st_out, Buf("x_stout")), ST)
            self.S.flush()
            self.setup = True

def kc_layout(w):
    K, N = w.shape
    return np.ascontiguousarray(w.reshape(K // 128, 128, N).transpose(1, 0, 2))


def col_layout(v):
    return np.ascontiguousarray(v.reshape(-1, 128).T)


def host_weights(inp):
    Wd = {}
    Wd["c_ident"] = np.eye(128, dtype=np.float32)
    Wd["c_masku"] = np.triu(np.ones((128, 128), np.float32))
    for i in range(DEPTH):
        g = inp["norm_gains"][i]
        Wd["gcol%d" % i] = np.ascontiguousarray(g.reshape(6, 8, 128).transpose(2, 0, 1))
        Wd["grow%d" % i] = np.ascontiguousarray(g)
        Wd["memg%d" % i] = col_layout(inp["mem_norm_gains"][i])
        Wd["xa_wq%d" % i] = kc_layout(inp["xa_wq"][i])
        Wd["xa_wkv%d" % i] = kc_layout(inp["xa_wkv"][i])
        Wd["xa_wo%d" % i] = kc_layout(inp["xa_wo"][i])
        Wd["ffn_win%d" % i] = kc_layout(inp["ffn_w_in"][i])
        Wd["ffn_wout%d" % i] = kc_layout(inp["ffn_w_out"][i])
    for j in range(2):
        Wd["gm_win%d" % j] = kc_layout(inp["gm_w_in"][j])
        Wd["gm_wout%d" % j] = kc_layout(inp["gm_w_out"][j])
        Wd["gm_wsT%d" % j] = np.ascontiguousarray(inp["gm_w_s"][j].transpose(2, 0, 1))
        Wd["gm_bin%d" % j] = np.ascontiguousarray(inp["gm_b_in"][j][None, :])
        Wd["gm_bucol%d" % j] = col_layout(inp["gm_b_in"][j][:1024])
        Wd["gm_lng%d" % j] = np.ascontiguousarray(inp["gm_ln_g"][j][None, :])
        Wd["gm_lnb%d" % j] = np.ascontiguousarray(inp["gm_ln_b"][j][None, :])
        Wd["gm_bs%d" % j] = np.ascontiguousarray(inp["gm_b_s"][j][None, :, :])

    m = np.triu(np.ones((128, 128), np.float32))
    Wd["c_tri_incl"] = m
    Wd["c_tri_excl"] = np.triu(np.ones((128, 128), np.float32), 1)
    Wd["c_tri_rem"] = np.tril(np.ones((128, 128), np.float32), -1)
    rep4 = lambda a: np.ascontiguousarray(a)
    ii = np.arange(128)
    for nm, kbig in (("o16", 16), ("o32", 32), ("o64", 64), ("o128", 128)):
        Wd["c_hm_" + nm] = ((ii[:, None] // kbig == ii[None, :] // kbig) & (ii[:, None] // (kbig // 2) != ii[None, :] // (kbig // 2))).astype(np.float32)
    Wd["c_hm_d8"] = (ii[:, None] // 8 == ii[None, :] // 8).astype(np.float32)
    Wd["c_mask_su4"] = rep4(np.triu(np.ones((128, 128), np.float32), 1))
    Wd["c_mask_iu4"] = rep4(np.triu(np.ones((128, 128), np.float32)))
    Wd["c_mask_sl4"] = rep4(np.tril(np.ones((128, 128), np.float32), -1))
    for j in range(2):
        Wd["rw_mixcol%d" % j] = np.ascontiguousarray(inp["rw_mix"][j].reshape(6, 8, 128).transpose(2, 0, 1))
        for n, p in (("wr", 0), ("wk", 1), ("wv", 2)):
            Wd["rw_%s%d" % (n, j)] = kc_layout(inp["rw_w_rkv"][j][p])
        Wd["rw_wo%d" % j] = kc_layout(inp["rw_w_o"][j])
        Wd["rw_w1%d" % j] = kc_layout(inp["rw_w1"][j]); Wd["rw_a1%d" % j] = kc_layout(inp["rw_a1"][j]); Wd["rw_g1%d" % j] = kc_layout(inp["rw_g1"][j])
        Wd["rw_w2%d" % j] = np.ascontiguousarray(inp["rw_w2"][j]); Wd["rw_a2%d" % j] = np.ascontiguousarray(inp["rw_a2"][j])
        Wd["rw_g2%d" % j] = np.ascontiguousarray(inp["rw_g2"][j])
        Wd["rw_w0%d" % j] = np.ascontiguousarray(inp["rw_w0"][j][None, :]); Wd["rw_a0%d" % j] = np.ascontiguousarray(inp["rw_a0"][j][None, :])
        Wd["rw_kk%d" % j] = np.ascontiguousarray(inp["rw_k_k"][j][None, :]); Wd["rw_ka%d" % j] = np.ascontiguousarray(inp["rw_k_a"][j][None, :])
        Wd["rw_rk%d" % j] = np.ascontiguousarray(inp["rw_r_k"][j].reshape(1, 1024))
        Wd["rw_lng%d" % j] = np.ascontiguousarray(inp["rw_ln_g"][j][None, :]); Wd["rw_lnb%d" % j] = np.ascontiguousarray(inp["rw_ln_b"][j][None, :])
    Wd["rw_v1"] = kc_layout(inp["rw_v1"][0]); Wd["rw_v2"] = np.ascontiguousarray(inp["rw_v2"][0]); Wd["rw_v0"] = np.ascontiguousarray(inp["rw_v0"][0][None, :])
    return Wd


def build_program(plan, TOK):
    B = Builder(TOK)
    nc = B.nc
    x_in = B.inp("x_in", [TOK, D]) if plan[0][0] not in ("rwkv", "rwkvp") else None
    x_out = B.outp("x_out", [TOK, D])
    with contextlib.ExitStack() as st:
        blk = st.enter_context(nc.Block())

        def body(_):
            with contextlib.ExitStack() as st2:
                B.start(st2)
                src = ("in", x_in)
                xscr = B.outp("xscr", [TOK, D]) if len(plan) > 1 else None
                for pi, (kind, layer) in enumerate(plan):
                    dst = ("out", x_out) if pi == len(plan) - 1 else ("scr", xscr)
                    if kind == "gmlp":
                        B.phase_gmlp(layer, src, dst)
                    elif kind == "xattn":
                        B.phase_xattn(layer, src, dst)
                    elif kind == "ffn":
                        B.phase_ffn(layer, src, dst)
                    elif kind == "rwkv":
                        B.phase_rwkv(layer, dst, seq=True)
                    elif kind == "rwkvp":
                        B.phase_rwkv(layer, dst, seq=False)
                    else:
                        raise ValueError(kind)
                    src = dst
                B.S.flush()
                B.S.final_wait("sp")
                print("sched: ops", len(B.S.ops), "waits", B.S.nwait, "cnt", B.S.ecnt, "dma", B.S.ndma, "dsems", len(B.S.dsem))

        blk.sync(body)
    return B


def run_plan(plan, TOK, xs, mems, Wd, extra=None):
    B = build_program(plan, TOK)
    in_maps = []
    for c in range(NCORES):
        m = {}
        for name, (ap, shape) in B.din.items():
            if name == "x_in":
                m[name] = xs[c]
            elif name == "mem":
                m[name] = mems[c]
            elif extra is not None and name in extra:
                m[name] = extra[name][c]
            else:
                a = Wd[name]
                assert tuple(a.shape) == tuple(shape), (name, a.shape, shape)
                m[name] = a
        in_maps.append(m)
    res = run_bass_kernel_spmd(B.nc, in_maps, core_ids=list(range(NCORES)))
    return res.results


def _rwkv_layer(layer, xseg, vf, per, TOK, mems, Wd):
    xprev = []
    for c in range(NCORES):
        xprev.append(np.zeros((128, D), np.float32) if c % per == 0 else np.ascontiguousarray(xseg[c - 1][-128:, :]))
    zero_state = np.zeros((64, 16, 64), np.float32)
    last_out = [None] * NCORES
    x_new = [None] * NCORES
    vf_new = [None] * NCORES
    for q in range(per):
        st_in = []
        for c in range(NCORES):
            if c % per == q and q > 0:
                st_in.append(np.ascontiguousarray(last_out[c - 1]))
            else:
                st_in.append(zero_state)
        extra = {"xin": xseg, "xprev": xprev, "st_in": st_in}
        if vf is not None:
            extra["vf_in"] = vf
        res = run_plan([("rwkv", layer)], TOK, xseg, mems, Wd, extra)
        for c in range(NCORES):
            last_out[c] = res[c]["st_out"]
            if c % per == q:
                x_new[c] = res[c]["x_out"]
                if vf is None:
                    vf_new[c] = res[c]["vf_out"]
    return x_new, vf_new


def _pad4(segs, c, per, TOK):
    b, q = c // per, c % per
    out = np.zeros((4 * TOK, D), np.float32)
    for qq in range(q + 1):
        out[(3 - q + qq) * TOK:(4 - q + qq) * TOK, :] = segs[b * per + qq]
    return out


def kernel(**inputs):
    inp = {k: np.asarray(v) for k, v in inputs.items()}
    x = inp["x"]
    Bsz, S, _ = x.shape
    per = NCORES // Bsz
    TOK = S // per
    xs = [np.ascontiguousarray(x[c // per, (c % per) * TOK:(c % per + 1) * TOK, :]) for c in range(NCORES)]
    mems = [np.ascontiguousarray(inp["mem"][c // per]) for c in range(NCORES)]
    Wd = host_weights(inp)
    res = run_plan([("gmlp", 0), ("xattn", 0), ("ffn", 0)], TOK, xs, mems, Wd)
    x1 = [res[c]["x_out"] for c in range(NCORES)]
    extra = {"xin4": [_pad4(x1, c, per, TOK) for c in range(NCORES)]}
    res = run_plan([("rwkvp", 1), ("xattn", 1), ("ffn", 1), ("gmlp", 2), ("xattn", 2), ("ffn", 2)], TOK, x1, mems, Wd, extra)
    x3 = [res[c]["x_out"] for c in range(NCORES)]
    vf = [res[c]["vf_out"] for c in range(NCORES)]
    extra = {"xin4": [_pad4(x3, c, per, TOK) for c in range(NCORES)], "vf4": [_pad4(vf, c, per, TOK) for c in range(NCORES)]}
    res = run_plan([("rwkvp", 3), ("xattn", 3), ("ffn", 3)], TOK, x3, mems, Wd, extra)
    out = np.empty_like(x)
    for c in range(NCORES):
        out[c // per, (c % per) * TOK:(c % per + 1) * TOK, :] = res[c]["x_out"]
    return out
```
